# Optimizing a Trainium2 kernel written in Bass

```python
import jax, jax.numpy as jnp
from jax import lax
import numpy as np

D_MODEL = 2048
BATCH = 32
SEQ = 256
DEPTH = 1
DEC_BATCH = 8
DEC_SEQ = 1024
PAST_LEN = 256

GRID_W = 64
H_MLA = 16
NOPE_DIM = 128
ROPE_DIM = 64
QK_HEAD = NOPE_DIM + ROPE_DIM
V_HEAD = 128
KV_RANK = 512
ROPE_THETA = 10000.0
AXIS_DIM = ROPE_DIM // 2
AXIS_PAIRS = AXIS_DIM // 2
Q_BLOCK = 128
MLA_WIDTH = H_MLA * V_HEAD
Q_DIM = H_MLA * QK_HEAD
HEAD_RWKV = 64
H_RWKV = D_MODEL // HEAD_RWKV
R_DIM = H_RWKV * HEAD_RWKV
W_LORA = 64
A_LORA = 64
G_LORA = 128
CONV_W = 3
N_DIR = 2
LNX_EPS = 64e-5
D_FF = 4 * D_MODEL
EPS = 1e-6
IN_SIZES = (Q_DIM, KV_RANK, ROPE_DIM, 3 * R_DIM, N_DIR * W_LORA, N_DIR * A_LORA, G_LORA, 2 * D_MODEL)
IN_DIM = Q_DIM + KV_RANK + ROPE_DIM + 3 * R_DIM + N_DIR * W_LORA + N_DIR * A_LORA + G_LORA + 2 * D_MODEL

kernel_name = 'hybrid_mla_rwkv7_dit_step'


def rms_norm(x, g):
    xf = x.astype(jnp.float32)
    y = xf * lax.rsqrt(jnp.mean(xf * xf, axis=-1, keepdims=True) + EPS)
    return (y * g.astype(jnp.float32)).astype(x.dtype)


def split_cols(z, sizes):
    out, off = [], 0
    for s in sizes:
        out.append(z[..., off:off + s])
        off += s
    return out


def rotate_pairs(x, ang):
    cos = jnp.cos(ang).astype(x.dtype)[None, :, None, :]
    sin = jnp.sin(ang).astype(x.dtype)[None, :, None, :]
    x1, x2 = x[..., :AXIS_PAIRS], x[..., AXIS_PAIRS:]
    return jnp.concatenate([x1 * cos - x2 * sin, x2 * cos + x1 * sin], axis=-1)


def axial_rope(x):
    T = x.shape[1]
    rows = T // GRID_W
    row = jnp.repeat(jnp.arange(rows, dtype=jnp.float32), GRID_W)
    col = jnp.tile(jnp.arange(GRID_W, dtype=jnp.float32), rows)
    inv_freq = jnp.power(ROPE_THETA, -jnp.arange(AXIS_PAIRS, dtype=jnp.float32) / AXIS_PAIRS)
    x_nope = x[..., :NOPE_DIM]
    x_row = x[..., NOPE_DIM:NOPE_DIM + AXIS_DIM]
    x_col = x[..., NOPE_DIM + AXIS_DIM:]
    return jnp.concatenate([x_nope, rotate_pairs(x_row, row[:, None] * inv_freq),
                            rotate_pairs(x_col, col[:, None] * inv_freq)], axis=-1)


def mla_keys_values(ckv, kr, p):
    B, L, _ = ckv.shape
    kv = (rms_norm(ckv, p['kv_norm']) @ p['w_kv_up']).reshape(B, L, H_MLA, NOPE_DIM + V_HEAD)
    k_rope = jnp.broadcast_to(kr[:, :, None, :], (B, L, H_MLA, ROPE_DIM))
    k = rms_norm(jnp.concatenate([kv[..., :NOPE_DIM], k_rope], axis=-1), p['k_norm'])
    return k, kv[..., NOPE_DIM:]


def block_attention(q, k, v):
    B, T, H, _ = q.shape
    nb = T // Q_BLOCK
    qb = jnp.moveaxis(q.reshape(B, nb, Q_BLOCK, H, QK_HEAD), 1, 0)
    scale = QK_HEAD ** -0.5

    def one_block(q_blk):
        s = jnp.einsum('bqhd,bkhd->bhqk', q_blk, k).astype(jnp.float32) * scale
        pr = jax.nn.softmax(s, axis=-1).astype(v.dtype)
        return jnp.einsum('bhqk,bkhd->bqhd', pr, v)

    o = lax.map(one_block, qb)
    return jnp.moveaxis(o, 0, 1).reshape(B, T, H * V_HEAD)


def centred_conv(x, w):
    xp = jnp.pad(x, ((0, 0), (1, 1), (0, 0)))
    return xp[:, :-2] * w[0] + xp[:, 1:-1] * w[1] + xp[:, 2:] * w[2]


def l2_normalize(x):
    xf = x.astype(jnp.float32)
    return (xf * lax.rsqrt(jnp.sum(xf * xf, axis=-1, keepdims=True) + 1e-12)).astype(x.dtype)


def rwkv7_scan(r, decay, k, v, kk, a, s0, reverse):
    seq = tuple(jnp.swapaxes(t.astype(jnp.float32), 0, 1) for t in (r, decay, k, v, kk, a))

    def step(S, inp):
        r_t, w_t, k_t, v_t, kk_t, a_t = inp
        s_kk = jnp.einsum('bhvk,bhk->bhv', S, kk_t)
        S = (S * w_t[:, :, None, :]
             - jnp.einsum('bhv,bhk->bhvk', s_kk, kk_t * a_t)
             + jnp.einsum('bhv,bhk->bhvk', v_t, k_t))
        return S, jnp.einsum('bhvk,bhk->bhv', S, r_t)

    s_fin, ys = lax.scan(step, s0.astype(jnp.float32), seq, reverse=reverse)
    return jnp.swapaxes(ys, 0, 1), s_fin


def head_group_norm(y, w, b):
    mu = jnp.mean(y, axis=-1, keepdims=True)
    var = jnp.mean(jnp.square(y - mu), axis=-1, keepdims=True)
    yn = ((y - mu) * lax.rsqrt(var + LNX_EPS)).reshape(y.shape[:2] + (R_DIM,))
    return yn * w.astype(jnp.float32) + b.astype(jnp.float32)


def rwkv7_branch(z_rkv, z_wd, z_ad, z_gd, p, s0):
    B, T, _ = z_rkv.shape
    rkv = centred_conv(z_rkv, p['conv_rkv'])
    r, k, v = [t.reshape(B, T, H_RWKV, HEAD_RWKV) for t in jnp.split(rkv, 3, axis=-1)]
    kk = l2_normalize(k * p['k_k'].reshape(H_RWKV, HEAD_RWKV))
    g = jax.nn.sigmoid(z_gd) @ p['g_up']
    wd = z_wd.reshape(B, T, N_DIR, W_LORA)
    ad = z_ad.reshape(B, T, N_DIR, A_LORA)
    k_a = p['k_a'].reshape(H_RWKV, HEAD_RWKV)
    ys, bonuses, finals = [], [], []
    for d in range(N_DIR):
        w_log = -jax.nn.softplus(-(p['w0'][d] + jnp.tanh(wd[:, :, d]) @ p['w_up'][d])) - 0.5
        decay = jnp.exp(-jnp.exp(w_log.astype(jnp.float32))).reshape(B, T, H_RWKV, HEAD_RWKV)
        a = jax.nn.sigmoid(p['a0'][d] + ad[:, :, d] @ p['a_up'][d]).reshape(B, T, H_RWKV, HEAD_RWKV)
        k_d = k * (1.0 + (a - 1.0) * k_a)
        y, s_fin = rwkv7_scan(r, decay, k_d, v, kk, a, s0[:, d], reverse=(d == 1))
        ys.append(y)
        bonuses.append(jnp.sum(r * k_d * p['r_k'], axis=-1, keepdims=True) * v)
        finals.append(s_fin)
    o = head_group_norm(ys[0] + ys[1], p['lnx_w'], p['lnx_b']).astype(z_rkv.dtype)
    o = o + (bonuses[0] + bonuses[1]).reshape(B, T, R_DIM)
    return o * g, jnp.stack(finals, axis=1)


def trunk_layer(x, cond, p, cache):
    B, T, _ = x.shape
    mod = (jax.nn.silu(cond) @ p['w_ada'] + p['b_ada']).reshape(-1, 1, 6 * D_MODEL)
    shift1, scale1, gate1, shift2, scale2, gate2 = jnp.split(mod, 6, axis=-1)
    h = rms_norm(x, p['norm1']) * (1.0 + scale1) + shift1
    z_q, z_ckv, z_kr, z_rkv, z_wd, z_ad, z_gd, z_gate = split_cols(h @ p['w_in'], IN_SIZES)
    q = rms_norm(z_q.reshape(B, T, H_MLA, QK_HEAD), p['q_norm'])
    k, v = mla_keys_values(z_ckv, z_kr, p)
    if cache is None:
        s0 = jnp.zeros((B, N_DIR, H_RWKV, HEAD_RWKV, HEAD_RWKV), jnp.float32)
    else:
        ckv_ctx, kr_ctx, s0 = cache
        q = axial_rope(q)
        k = axial_rope(k)
        k_ctx, v_ctx = mla_keys_values(ckv_ctx, kr_ctx, p)
        k = jnp.concatenate([k, k_ctx], axis=1)
        v = jnp.concatenate([v, v_ctx], axis=1)
    o_mla = block_attention(q, k, v)
    o_rwkv, s_final = rwkv7_branch(z_rkv, z_wd, z_ad, z_gd, p, s0)
    gates = jax.nn.sigmoid(z_gate)
    merged = (gates[..., :D_MODEL] * (o_mla @ p['w_br_mla'])
              + gates[..., D_MODEL:] * (o_rwkv @ p['w_br_rwkv']))
    x = x + gate1 * (merged @ p['w_out'])
    h2 = rms_norm(x, p['norm2']) * (1.0 + scale2) + shift2
    x = x + gate2 * (jnp.square(jax.nn.relu(h2 @ p['w_ff_in'])) @ p['w_ff_out'])
    return x, (z_ckv, z_kr, s_final)


def setup_inputs(seed: int = 0) -> dict:
    key = jax.random.key(seed)
    ks = iter(jax.random.split(key, 40))

    def nrm(shape, scale):
        return scale * jax.random.normal(next(ks), shape, jnp.float32)

    L = DEPTH
    R = R_DIM
    conv = nrm((L, CONV_W, 3 * R), 0.2).at[:, CONV_W // 2].add(1.0)
    return {
        'x_prompt': nrm((BATCH, SEQ, D_MODEL), 1.0),
        'x_sample': nrm((DEC_BATCH, DEC_SEQ, D_MODEL), 1.0),
        'cache_mla_ckv': nrm((DEC_BATCH, L, PAST_LEN, KV_RANK), 1.0),
        'cache_mla_kr': nrm((DEC_BATCH, L, PAST_LEN, ROPE_DIM), 1.0),
        'state_rwkv': nrm((DEC_BATCH, L, N_DIR, H_RWKV, HEAD_RWKV, HEAD_RWKV), 1.0),
        'c': nrm((DEC_BATCH, D_MODEL), 1.0),
        'c_ctx': nrm((D_MODEL,), 1.0),
        'norm1': 1.0 + nrm((L, D_MODEL), 0.05),
        'w_ada': nrm((L, D_MODEL, 6 * D_MODEL), 0.01),
        'b_ada': nrm((L, 6 * D_MODEL), 0.1),
        'w_in': nrm((L, D_MODEL, IN_DIM), D_MODEL ** -0.5),
        'q_norm': 1.0 + nrm((L, QK_HEAD), 0.05),
        'kv_norm': 1.0 + nrm((L, KV_RANK), 0.05),
        'w_kv_up': nrm((L, KV_RANK, H_MLA * (NOPE_DIM + V_HEAD)), KV_RANK ** -0.5),
        'k_norm': 1.0 + nrm((L, QK_HEAD), 0.05),
        'conv_rkv': conv,
        'k_k': 0.85 + nrm((L, R), 0.05),
        'k_a': 1.0 + nrm((L, R), 0.05),
        'r_k': nrm((L, H_RWKV, HEAD_RWKV), 0.1),
        'w0': -1.5 + nrm((L, N_DIR, R), 0.5),
        'w_up': nrm((L, N_DIR, W_LORA, R), 0.5 * W_LORA ** -0.5),
        'a0': nrm((L, N_DIR, R), 0.1),
        'a_up': nrm((L, N_DIR, A_LORA, R), A_LORA ** -0.5),
        'g_up': nrm((L, G_LORA, R), G_LORA ** -0.5),
        'lnx_w': 1.0 + nrm((L, R), 0.05),
        'lnx_b': nrm((L, R), 0.01),
        'w_br_mla': nrm((L, MLA_WIDTH, D_MODEL), MLA_WIDTH ** -0.5),
        'w_br_rwkv': nrm((L, R, D_MODEL), R ** -0.5),
        'w_out': nrm((L, D_MODEL, D_MODEL), D_MODEL ** -0.5),
        'norm2': 1.0 + nrm((L, D_MODEL), 0.05),
        'w_ff_in': nrm((L, D_MODEL, D_FF), D_MODEL ** -0.5),
        'w_ff_out': nrm((L, D_FF, D_MODEL), D_FF ** -0.5),
    }


def reference(x_prompt, x_sample, cache_mla_ckv, cache_mla_kr, state_rwkv, c, c_ctx,
              norm1, w_ada, b_ada, w_in, q_norm, kv_norm, w_kv_up, k_norm, conv_rkv,
              k_k, k_a, r_k, w0, w_up, a0, a_up, g_up, lnx_w, lnx_b,
              w_br_mla, w_br_rwkv, w_out, norm2, w_ff_in, w_ff_out):
    x_p = x_prompt
    x_s = x_sample
    ckv_list, kr_list, st_list = [], [], []
    for l in range(DEPTH):
        p = {
            'norm1': norm1[l], 'w_ada': w_ada[l], 'b_ada': b_ada[l], 'w_in': w_in[l],
            'q_norm': q_norm[l], 'kv_norm': kv_norm[l], 'w_kv_up': w_kv_up[l], 'k_norm': k_norm[l],
            'conv_rkv': conv_rkv[l], 'k_k': k_k[l], 'k_a': k_a[l], 'r_k': r_k[l],
            'w0': w0[l], 'w_up': w_up[l], 'a0': a0[l], 'a_up': a_up[l], 'g_up': g_up[l],
            'lnx_w': lnx_w[l], 'lnx_b': lnx_b[l], 'w_br_mla': w_br_mla[l], 'w_br_rwkv': w_br_rwkv[l],
            'w_out': w_out[l], 'norm2': norm2[l], 'w_ff_in': w_ff_in[l], 'w_ff_out': w_ff_out[l],
        }
        x_p, (ckv_l, kr_l, st_l) = trunk_layer(x_p, c_ctx, p, None)
        ckv_list.append(ckv_l)
        kr_list.append(kr_l)
        st_list.append(st_l)
        x_s, _ = trunk_layer(x_s, c, p, (cache_mla_ckv[:, l], cache_mla_kr[:, l], state_rwkv[:, l]))
    new_cache_mla_ckv = jnp.stack(ckv_list, axis=1)
    new_cache_mla_kr = jnp.stack(kr_list, axis=1)
    new_state_rwkv = jnp.stack(st_list, axis=1)
    return (x_p, x_s, new_cache_mla_ckv, new_cache_mla_kr, new_state_rwkv)
```

```python
from contextlib import ExitStack
import numpy as np
import concourse.bass as bass
import concourse.mybir as mybir
from concourse.bass_utils import run_bass_kernel_spmd

F32 = mybir.dt.float32
BF16 = mybir.dt.bfloat16
AF = mybir.ActivationFunctionType
ALU = mybir.AluOpType

ENGS = ("pe", "act", "dve", "pool", "sp")
N_DMA_SEMS = 28
SEM_CH = 16000
N_ENG_SEMS = 8


class Sched:
    def __init__(self):
        self.streams = {e: [] for e in ENGS}
        self.count = {e: 0 for e in ENGS}
        self.known = {e: {} for e in ENGS}
        self.last_w = {}
        self.readers = {}
        self.dma_i = 0
        self.dma_qi = {"sp": 0, "pool": 0}
        self.dma_cnt = [0] * (2 * N_DMA_SEMS)
        self.dma_bank = 0

    def _deps(self, reads, writes):
        toks = []
        for r in reads:
            t = self.last_w.get(r)
            if t is not None:
                toks.append(t)
        for w in writes:
            t = self.last_w.get(w)
            if t is not None:
                toks.append(t)
            toks.extend(self.readers.get(w, ()))
        return toks

    def _commit(self, tok, reads, writes):
        for r in reads:
            self.readers.setdefault(r, []).append(tok)
        for w in writes:
            self.last_w[w] = tok
            self.readers[w] = []

    def _waits(self, eng, toks):
        need = {}
        kn = self.known[eng]
        for (k, v) in toks:
            if kn.get(k, 0) >= v:
                continue
            if need.get(k, 0) < v:
                need[k] = v
        for k, v in need.items():
            kn[k] = v
        return list(need.items())

    def op(self, eng, fn, reads=(), writes=()):
        toks = self._deps(reads, writes)
        waits = self._waits(eng, toks)
        n = self.count[eng]
        self.count[eng] += 1
        key = (eng, n // SEM_CH)
        tok = (key, n % SEM_CH + 1)
        self.streams[eng].append((waits, fn, (key, 1)))
        self._commit(tok, reads, writes)
        return tok

    def dma(self, eng, fn, reads=(), writes=()):
        toks = self._deps(reads, writes)
        qb = 0 if eng == "sp" else 1
        s = qb * N_DMA_SEMS + self.dma_qi[eng] % N_DMA_SEMS
        self.dma_qi[eng] += 1
        self.dma_i += 1
        key = ("dma", s)
        if self.dma_cnt[s] > 0:
            toks.append((key, 16 * self.dma_cnt[s]))
        waits = self._waits(eng, toks)
        self.dma_cnt[s] += 1
        tok = (key, 16 * self.dma_cnt[s])
        self.streams[eng].append((waits, fn, (key, 16)))
        self._commit(tok, reads, writes)
        return tok

    def _all_toks(self):
        toks = [((e, (self.count[e] - 1) // SEM_CH), (self.count[e] - 1) % SEM_CH + 1)
                for e in ENGS if self.count[e] > 0]
        toks += [(("dma", s), 16 * c) for s, c in enumerate(self.dma_cnt) if c > 0]
        return toks

    def barrier(self):
        toks = self._all_toks()
        for e in ENGS:
            waits = self._waits(e, toks)
            if waits:
                self.streams[e].append((waits, None, None))

    def finish(self, eng="sp"):
        waits = self._waits(eng, self._all_toks())
        if waits:
            self.streams[eng].append((waits, None, None))

    def clear_streams(self):
        self.streams = {e: [] for e in ENGS}

    def emit(self, sems, block):
        def run(engobj, ename):
            for waits, fn, inc in self.streams[ename]:
                for k, v in waits:
                    engobj.wait_ge(sems[k], v)
                if fn is not None:
                    ins = fn(engobj)
                    ins.then_inc(sems[inc[0]], inc[1])

        @block.tensor
        def _(e):
            run(e, "pe")

        @block.scalar
        def _(e):
            run(e, "act")

        @block.vector
        def _(e):
            run(e, "dve")

        @block.gpsimd
        def _(e):
            run(e, "pool")

        @block.sync
        def _(e):
            run(e, "sp")


D = 2048
NT = 1024
KC = 16
H_MLA = 16
KV_RANK = 512
IN_DIM = 14272
OFF_Q, OFF_CKV, OFF_KR, OFF_RKV, OFF_WD, OFF_AD, OFF_GD, OFF_GATE = 0, 3072, 3584, 3648, 9792, 9920, 10048, 10176
EPS = 1e-6
LNX_EPS = 64e-5
C = 64
NCH = NT // C
SQ2048 = float(np.sqrt(2048.0))

CT_IDENT, CT_MASK0, CT_MASK1, CT_BONES, CT_BAVG, CT_SCAN, CT_ROPEC, CT_ROPES, CT_EPS, CT_PERM, CT_END = (
    0, 128, 448, 768, 896, 1024, 2048, 3072, 4096, 4104, 4168)


def make_consts():
    ct = np.zeros((128, CT_END), np.float32)
    ct[:, CT_IDENT:CT_IDENT + 128] = np.eye(128, dtype=np.float32)
    idx = np.arange(64)
    for d, base in ((0, CT_MASK0), (1, CT_MASK1)):
        if d == 0:
            t_gt_i_rows_t = (idx[:, None] > idx[None, :])
            t_gt_i_rows_i = (idx[None, :] > idx[:, None])
            t_ge_i_rows_i = (idx[None, :] >= idx[:, None])
        else:
            t_gt_i_rows_t = (idx[:, None] < idx[None, :])
            t_gt_i_rows_i = (idx[None, :] < idx[:, None])
            t_ge_i_rows_i = (idx[None, :] <= idx[:, None])
        blocks = [-1.0 * t_gt_i_rows_t, -1.0 * t_gt_i_rows_i, 1.0 * t_gt_i_rows_i,
                  1.0 * t_ge_i_rows_i, -1.0 * t_ge_i_rows_i]
        m = np.concatenate([b.astype(np.float32) for b in blocks], axis=1)
        ct[0:64, base:base + 320] = m
        ct[64:128, base:base + 320] = m
    bo = np.zeros((128, 128), np.float32)
    bo[0:64, 0:64] = 1.0
    bo[64:128, 64:128] = 1.0
    ct[:, CT_BONES:CT_BONES + 128] = bo
    ct[:, CT_BAVG:CT_BAVG + 128] = bo / 64.0
    sm = np.ones(1024, np.float32)
    sm[::64] = 0.0
    ct[:, CT_SCAN:CT_SCAN + 1024] = sm[None, :]
    t = np.arange(1024)
    row = (t // 64).astype(np.float32)
    col = (t % 64).astype(np.float32)
    inv = np.power(np.float32(10000.0), -np.arange(16, dtype=np.float32) / np.float32(16)).astype(np.float32)
    ar = (row[:, None] * inv[None, :]).astype(np.float32)
    ac = (col[:, None] * inv[None, :]).astype(np.float32)
    cr, sr, cc, sc = np.cos(ar), np.sin(ar), np.cos(ac), np.sin(ac)
    Ct = np.concatenate([cr, cr, cc, cc], axis=1).astype(np.float32)
    St = np.concatenate([-sr, sr, -sc, sc], axis=1).astype(np.float32)
    ct[0:64, CT_ROPEC:CT_ROPEC + 1024] = Ct.T
    ct[0:64, CT_ROPES:CT_ROPES + 1024] = St.T
    pm = np.zeros((64, 64), np.float32)
    for dd in range(64):
        g0 = (dd // 32) * 32
        o = dd - g0
        pm[g0 + (o + 16) % 32, dd] = 1.0
    ct[0:64, CT_PERM:CT_PERM + 64] = pm
    ct[:, CT_EPS + 0] = 2048.0 * EPS
    ct[:, CT_EPS + 1] = EPS
    ct[:, CT_EPS + 2] = 1e-12
    ct[:, CT_EPS + 3] = LNX_EPS
    ct[:, CT_EPS + 4] = 192.0 * EPS
    return ct


ARENA_WORDS = 53120


class Prog:
    def __init__(self, debug=()):
        self.debug = set(debug)
        self.nc = bass.Bass("TRN2", target_bir_lowering=False)
        self.S = Sched()
        self.top = 0
        self.limit = ARENA_WORDS
        self.dbg_outs = {}

    def alloc(self, dtype, shape):
        n = int(np.prod(shape))
        words = n if dtype == F32 else (n + 1) // 2
        words = (words + 7) // 8 * 8
        off = self.top
        self.top += words
        assert self.top <= self.limit, f"arena overflow {self.top} > {self.limit}"
        ap = self.arena[:, off:off + (n if dtype == F32 else (n + 1) // 2)]
        if dtype != F32:
            ap = ap.bitcast(BF16)
            if n % 2:
                ap = ap[:, 0:n]
        if len(shape) == 2:
            ap = ap.rearrange("p (a b) -> p a b", a=shape[0])
        elif len(shape) == 3:
            ap = ap.rearrange("p (a b c) -> p a b c", a=shape[0], b=shape[1])
        elif len(shape) == 4:
            ap = ap.rearrange("p (a b c d) -> p a b c d", a=shape[0], b=shape[1], c=shape[2])
        return ap

    def mark(self):
        return self.top

    def release(self, m, barrier=True):
        if barrier and self.top > m:
            self.S.barrier()
        self.top = m

    def dma(self, eng, out, in_, reads=(), writes=()):
        self.S.dma(eng, lambda e: e.dma_start(out=out, in_=in_), reads, writes)

    def act(self, out, in_, func, reads=(), writes=(), **kw):
        self.S.op("act", lambda e: e.activation(out=out, in_=in_, func=func, **kw), reads, writes)

    def tt(self, eng, out, in0, in1, op, reads=(), writes=()):
        self.S.op(eng, lambda e: e.tensor_tensor(out=out, in0=in0, in1=in1, op=op), reads, writes)

    def ts(self, eng, out, in0, s1, op0, s2=None, op1=None, reads=(), writes=(), accum_out=None):
        if op1 is None:
            if accum_out is None:
                self.S.op(eng, lambda e: e.tensor_scalar(out=out, in0=in0, scalar1=s1, scalar2=None, op0=op0), reads, writes)
            else:
                self.S.op(eng, lambda e: e.tensor_scalar(out=out, in0=in0, scalar1=s1, scalar2=None, op0=op0, accum_out=accum_out), reads, writes)
        else:
            self.S.op(eng, lambda e: e.tensor_scalar(out=out, in0=in0, scalar1=s1, scalar2=s2, op0=op0, op1=op1), reads, writes)

    def stt(self, eng, out, in0, scalar, in1, op0, op1, reads=(), writes=()):
        self.S.op(eng, lambda e: e.scalar_tensor_tensor(out=out, in0=in0, scalar=scalar, in1=in1, op0=op0, op1=op1), reads, writes)

    def copy(self, eng, out, in_, reads=(), writes=()):
        if eng == "act":
            self.S.op("act", lambda e: e.activation(out=out, in_=in_, func=AF.Copy), reads, writes)
        else:
            self.S.op(eng, lambda e: e.tensor_copy(out=out, in_=in_), reads, writes)

    def memset(self, eng, ap, val, writes=()):
        self.S.op(eng, lambda e: e.memset(ap, val), (), writes)

    def recip(self, out, in_, reads=(), writes=()):
        self.S.op("dve", lambda e: e.reciprocal(out=out, in_=in_), reads, writes)

    def pe(self, mms, reads=(), writes=()):
        mms = list(mms)

        def fn(e):
            ins = None
            for (o, l, r, st, sp, tp) in mms:
                if tp is None:
                    ins = e.matmul(o, l, r, start=st, stop=sp)
                else:
                    ins = e.matmul(o, l, r, start=st, stop=sp, tile_position=tp)
            return ins
        self.S.op("pe", fn, reads, writes)

    def dbg(self, name, ap_sb, shape, reads):
        if name not in self.debug:
            return
        d = self.nc.dram_tensor("dbg_" + name, [128] + list(shape), ap_sb.dtype, kind="ExternalOutput").ap()
        self.dbg_outs[name] = d
        self.dma("sp", d, ap_sb, reads=reads)


def build_program(debug=(), stop_after=None):
    P = Prog(debug)
    nc = P.nc
    S = P.S

    def din(name, shape):
        return nc.dram_tensor(name, shape, F32, kind="ExternalInput").ap()

    def dout(name, shape):
        return nc.dram_tensor(name, shape, F32, kind="ExternalOutput").ap()

    xin = [din("xp", [NT, D]), din("xs", [NT, D])]
    cckv = din("cckv", [256, 512])
    ckr = din("ckr", [256, 64])
    st_in = din("st", [2, 32, 64, 64])
    cond = din("cond", [2, D])
    ctab = din("ctab", [128, CT_END])
    norm1 = din("norm1", [D]); w_ada = din("w_ada", [D, 6 * D]); b_ada = din("b_ada", [6 * D])
    w_in = din("w_in", [D, IN_DIM]); q_norm = din("q_norm", [192]); kv_norm = din("kv_norm", [512])
    w_kv_up = din("w_kv_up", [512, 4096]); k_norm = din("k_norm", [192]); conv_rkv = din("conv_rkv", [3, 6144])
    k_k = din("k_k", [D]); k_a = din("k_a", [D]); r_k = din("r_k", [D])
    w0 = din("w0", [2, D]); w_up = din("w_up", [2, 64, D]); a0 = din("a0", [2, D]); a_up = din("a_up", [2, 64, D])
    g_up = din("g_up", [128, D]); lnx_w = din("lnx_w", [D]); lnx_b = din("lnx_b", [D])
    w_br_mla = din("w_br_mla", [D, D]); w_br_rwkv = din("w_br_rwkv", [D, D]); w_out = din("w_out", [D, D])
    norm2 = din("norm2", [D]); w_ff_in = din("w_ff_in", [D, 4 * D]); w_ff_out = din("w_ff_out", [4 * D, D])

    yout = [dout("yp", [NT, D]), dout("ys", [NT, D])]
    nckv = dout("nckv", [NT, 512])
    nkr = dout("nkr", [NT, 64])
    nst = dout("nst", [4, 2, 32, 64, 64])

    es = ExitStack()
    with es:
        keys = [(e, i) for e in ENGS for i in range(N_ENG_SEMS if e != "sp" else 1)] + [("dma", s) for s in range(2 * N_DMA_SEMS)]
        sems = {k: es.enter_context(nc.semaphore(f"s_{k[0]}_{k[1]}")) for k in keys}
        P.arena = es.enter_context(nc.sbuf_tensor("arena", [128, ARENA_WORDS], F32))
        psall = es.enter_context(nc.psum_tensor("psall", [128, 4096], F32))
        psb = [psall[:, 512 * i:512 * (i + 1)] for i in range(8)]

        def PSR(i):
            return ("ps", i)

        ct = P.alloc(F32, [CT_END])
        P.dma("sp", ct, ctab[:, :], writes=["ct"])
        identf = ct[:, CT_IDENT:CT_IDENT + 128]
        bones = ct[:, CT_BONES:CT_BONES + 128]
        bavg = ct[:, CT_BAVG:CT_BAVG + 128]
        scanmask = ct[:, CT_SCAN:CT_SCAN + 1024]
        ropeCT = ct[0:64, CT_ROPEC:CT_ROPEC + 1024]
        ropeST = ct[0:64, CT_ROPES:CT_ROPES + 1024]
        permf = ct[0:64, CT_PERM:CT_PERM + 64]
        maskd = [ct[:, CT_MASK0:CT_MASK0 + 320], ct[:, CT_MASK1:CT_MASK1 + 320]]
        eps2048 = ct[:, CT_EPS + 0:CT_EPS + 1]
        eps1 = ct[:, CT_EPS + 1:CT_EPS + 2]
        eps12 = ct[:, CT_EPS + 2:CT_EPS + 3]
        epslnx = ct[:, CT_EPS + 3:CT_EPS + 4]
        eps192 = ct[:, CT_EPS + 4:CT_EPS + 5]
        ident = P.alloc(BF16, [128])
        P.copy("dve", ident, identf, reads=["ct"], writes=["ident"])
        ones_bf = P.alloc(BF16, [128])
        P.memset("pool", ones_bf, 1.0, writes=["ones_bf"])
        I2 = P.alloc(F32, [64])
        P.tt("dve", I2, identf[:, 0:64], identf[:, 64:128], ALU.add, reads=["ct"], writes=["I2"])

        cv = {}
        stage = P.alloc(F32, [128])
        cvn = 0

        def colvec(name, dram_rows_ap, n):
            nonlocal cvn
            dst = P.alloc(F32, [n])
            P.dma("sp", stage[0:n, :], dram_rows_ap, reads=(), writes=["stage"])
            P.pe([(psb[7][:, 0:n], stage[0:n, :], identf[0:n, 0:n], True, True, None)],
                 reads=["stage", "ct"], writes=[PSR(7)])
            P.copy("dve", dst, psb[7][:, 0:n], reads=[PSR(7)], writes=["cv_" + name])
            cv[name] = dst
            cvn += 1

        colvec("b_ada", b_ada.rearrange("(n p) -> n p", p=128), 96)
        colvec("norm1", norm1.rearrange("(n p) -> n p", p=128), 16)
        colvec("norm2", norm2.rearrange("(n p) -> n p", p=128), 16)
        colvec("kv_norm", kv_norm.rearrange("(n p) -> n p", p=128), 4)
        for tap in range(3):
            colvec(f"conv{tap}", conv_rkv[tap].rearrange("(n p) -> n p", p=128), 48)
        colvec("k_k", k_k.rearrange("(n p) -> n p", p=128), 16)
        colvec("k_a", k_a.rearrange("(n p) -> n p", p=128), 16)
        colvec("r_k", r_k.rearrange("(n p) -> n p", p=128), 16)
        colvec("w0", w0.rearrange("d (n p) -> (d n) p", p=128), 32)
        colvec("a0", a0.rearrange("d (n p) -> (d n) p", p=128), 32)
        colvec("lnx_w", lnx_w.rearrange("(n p) -> n p", p=128), 16)
        colvec("lnx_b", lnx_b.rearrange("(n p) -> n p", p=128), 16)
        CVR = ["cv_" + k for k in cv]
        omka = P.alloc(F32, [16])
        P.ts("dve", omka, cv["k_a"], -1.0, ALU.mult, s2=1.0, op1=ALU.add, reads=["cv_k_a"], writes=["omka"])

        gcol = P.alloc(F32, [4])
        P.memset("dve", gcol, 0.0, writes=["gcol"])
        for ci_, (src_, n_) in enumerate(((q_norm[0:128], 128), (q_norm[128:192], 64), (k_norm[0:128], 128), (k_norm[128:192], 64))):
            P.dma("sp", stage[0:1, 0:n_], src_.rearrange("(o n) -> o n", o=1), writes=["stage"])
            P.pe([(psb[7][0:n_, 0:1], stage[0:1, 0:n_], identf[0:1, 0:1], True, True, None)], reads=["stage", "ct"], writes=[PSR(7)])
            P.copy("dve", gcol[0:n_, ci_:ci_ + 1], psb[7][0:n_, 0:1], reads=[PSR(7)], writes=["gcol"])
        P.ts("dve", gcol[:, 0:2], gcol[:, 0:2], float(192.0 ** -0.5), ALU.mult, reads=["gcol"], writes=["gcol"])

        modT = P.alloc(F32, [96, 2])
        scT = P.alloc(BF16, [KC, 2])
        g1 = P.alloc(F32, [2, KC]); g2 = P.alloc(F32, [2, KC])
        sh1 = P.alloc(F32, [2, KC]); sh2 = P.alloc(F32, [2, KC])
        m0 = P.mark()
        craw = P.alloc(F32, [D])
        P.dma("sp", craw[0:2, :], cond[:, :], writes=["craw"])
        P.act(craw[0:2, :], craw[0:2, :], AF.Silu, reads=["craw"], writes=["craw"])
        P.pe([(psb[7][:, 2 * kc:2 * kc + 2], craw[0:2, kc * 128:(kc + 1) * 128], identf[0:2, 0:2], True, True, None)
              for kc in range(KC)], reads=["craw", "ct"], writes=[PSR(7)])
        P.copy("dve", scT, psb[7][:, 0:2 * KC].rearrange("p (a b) -> p a b", a=KC), reads=[PSR(7)], writes=["scT"])
        wb = [P.alloc(BF16, [KC, 512]) for _ in range(2)]
        nblk = 0
        for sixth in range(6):
            for cb in range(4):
                col0 = sixth * D + cb * 512
                buf = wb[nblk % 2]
                br = ("wb", nblk % 2)
                P.dma("pool", buf, w_ada[:, col0:col0 + 512].rearrange("(kc p) n -> p kc n", p=128), writes=[br])
                mms = []
                for fb in range(4):
                    blk = (col0 // 128) + fb
                    for kc in range(KC):
                        mms.append((psb[6][:, 2 * blk:2 * blk + 2], buf[:, kc, fb * 128:(fb + 1) * 128], scT[:, kc, :],
                                    kc == 0, kc == KC - 1, None))
                P.pe(mms, reads=[br, "scT"], writes=[("modps", nblk)])
                nblk += 1
        P.memset("dve", modT, 0.0, writes=["modT"])
        for sixth in range(6):
            P.tt("dve", modT[:, sixth * 16:(sixth + 1) * 16, :], psb[6][:, sixth * 32:(sixth + 1) * 32].rearrange("p (a b) -> p a b", b=2),
                 cv["b_ada"][:, sixth * 16:(sixth + 1) * 16].unsqueeze(2).to_broadcast([128, 16, 2]), ALU.add,
                 reads=[("modps", i) for i in range(nblk)] + ["cv_b_ada"], writes=["modT"])
        for j in range(2):
            for (gdst, shdst, sixth_sh, sixth_sc, nrm) in ((g1, sh1, 0, 1, "norm1"), (g2, sh2, 3, 4, "norm2")):
                P.ts("dve", gdst[:, j, :], modT[:, sixth_sc * 16:(sixth_sc + 1) * 16, j], 1.0, ALU.add,
                     reads=["modT"], writes=["gsh"])
                P.tt("dve", gdst[:, j, :], gdst[:, j, :], cv[nrm], ALU.mult, reads=["gsh", "cv_" + nrm], writes=["gsh"])
                P.ts("dve", gdst[:, j, :], gdst[:, j, :], SQ2048, ALU.mult, reads=["gsh"], writes=["gsh"])
                P.copy("dve", shdst[:, j, :], modT[:, sixth_sh * 16:(sixth_sh + 1) * 16, j], reads=["modT"], writes=["gsh"])
        P.release(m0)
        S.barrier()
        S.dma_bank = 1
        P.dbg("modT", modT, [96, 2], ["modT"])
        P.dbg("g1", g1, [2, KC], ["gsh"])

        persist_top = P.mark()

        def run_group(gi):
            xg = xin[gi]
            yg = yout[gi]
            cj = gi
            nseq, T = ((4, 256) if gi == 0 else (1, 1024))
            has_cache = (gi == 1)
            P.release(persist_top)
            hT = P.alloc(BF16, [KC, NT])
            m_h = P.mark()

            def norm_to_T(src_tiles_fn, gcol, shcol, dstT, dst_region):
                mA = P.mark()
                xn = [P.alloc(BF16, [D]) for _ in range(2)]
                ssq = P.alloc(F32, [16])
                for ti in range(8):
                    xt, xr = src_tiles_fn(ti)
                    P.act(xn[ti % 2], xt, AF.Square, reads=[xr], writes=[("ssq", ti), ("xn", ti % 2)], accum_out=ssq[:, ti:ti + 1])
                    P.act(ssq[:, 8 + ti:9 + ti], ssq[:, ti:ti + 1], AF.Sqrt, reads=[("ssq", ti), "ct"], writes=[("ssq2", ti)],
                          bias=eps2048, scale=1.0)
                    P.recip(ssq[:, 8 + ti:9 + ti], ssq[:, 8 + ti:9 + ti], reads=[("ssq2", ti)], writes=[("ssq2", ti)])
                    xb = xn[ti % 2]
                    P.ts("dve", xb, xt, ssq[:, 8 + ti:9 + ti], ALU.mult, reads=[xr, ("ssq2", ti)], writes=[("xn", ti % 2)])
                    for q in range(4):
                        pst = psb[q % 2]
                        P.pe([(pst[:, j * 128:(j + 1) * 128], xb[:, (4 * q + j) * 128:(4 * q + j + 1) * 128], ident, True, True, None)
                              for j in range(4)], reads=[("xn", ti % 2), "ident"], writes=[PSR(q % 2)])
                        for j in range(4):
                            kc = 4 * q + j
                            if j % 2 == 0:
                                P.act(dstT[:, kc, ti * 128:(ti + 1) * 128], pst[:, j * 128:(j + 1) * 128], AF.Identity,
                                      reads=[PSR(q % 2), "gsh"], writes=[(dst_region, ti)],
                                      scale=gcol[:, kc:kc + 1], bias=shcol[:, kc:kc + 1])
                            else:
                                P.ts("dve", dstT[:, kc, ti * 128:(ti + 1) * 128], pst[:, j * 128:(j + 1) * 128], gcol[:, kc:kc + 1], ALU.mult,
                                     s2=shcol[:, kc:kc + 1], op1=ALU.add, reads=[PSR(q % 2), "gsh"], writes=[(dst_region, ti)])
                if f"ps{gi}" in P.debug:
                    pdump = P.alloc(F32, [512])
                    P.copy("dve", pdump, psb[1][:, :], reads=[PSR(1)], writes=["pdump"])
                    P.dbg(f"ps{gi}", pdump, [512], ["pdump"])
                    P.dbg(f"ident{gi}", ident, [128], ["ident"])
                P.dbg(f"ssq{gi}", ssq, [16], [("ssq2", t_) for t_ in range(8)])
                P.dbg(f"xn{gi}", xn[1], [D], [("xn", 1)])
                P.release(mA)

            mX = P.mark()
            xbuf = [P.alloc(F32, [D]) for _ in range(2)]

            def load_x(ti):
                P.dma("sp", xbuf[ti % 2], xg[ti * 128:(ti + 1) * 128, :], writes=[("xbuf", ti % 2)])
                return xbuf[ti % 2], ("xbuf", ti % 2)
            norm_to_T(load_x, g1[:, cj, :], sh1[:, cj, :], hT, "hT")
            P.release(mX)
            HT_R = [("hT", ti) for ti in range(8)]
            P.dbg(f"hT{gi}", hT, [KC, NT], HT_R)
            if stop_after == "A":
                return

            orwkvT = P.alloc(BF16, [16, NT])
            def phase_R():
                CPS = T // C
                mR = P.mark()
                twd_m = P.alloc(BF16, [NT])
                ad_m = P.alloc(BF16, [NT])
                sgdT = P.alloc(BF16, [NT])
                mW = P.mark()
                wl = P.alloc(BF16, [KC, 384])
                P.dma("pool", wl, w_in[:, OFF_WD:OFF_WD + 384].rearrange("(kc p) n -> p kc n", p=128), writes=["wl"])
                for cb in range(3):
                    for half in range(2):
                        ps = psb[half]
                        P.pe([(ps, wl[:, kc, cb * 128:(cb + 1) * 128], hT[:, kc, half * 512:(half + 1) * 512], kc == 0, kc == KC - 1, None)
                              for kc in range(KC)], reads=["wl"] + HT_R, writes=[PSR(half)])
                        hs = slice(half * 512, (half + 1) * 512)
                        if cb == 0:
                            P.act(twd_m[:, hs], ps, AF.Tanh, reads=[PSR(half)], writes=["twd_m"])
                        elif cb == 1:
                            P.act(ad_m[:, hs], ps, AF.Copy, reads=[PSR(half)], writes=["ad_m"])
                        else:
                            P.act(sgdT[:, hs], ps, AF.Sigmoid, reads=[PSR(half)], writes=["sgdT"])
                P.release(mW)
                wrk = [P.alloc(BF16, [KC, 128]) for _ in range(3)]
                wsm2 = [P.alloc(BF16, [5, 128]) for _ in range(2)]
                zpad = P.alloc(F32, [3, nseq * (T + 2)])
                T456 = [P.alloc(F32, [NT]) for _ in range(3)]
                rkv = P.alloc(F32, [3, NT])
                kk = P.alloc(BF16, [NT]); ysum = P.alloc(F32, [NT])
                vtm_b = [P.alloc(BF16, [NCH, 64]) for _ in range(2)]
                fmKR = [P.alloc(BF16, [2, NT]) for _ in range(2)]
                fmKB0 = P.alloc(BF16, [2, NT])
                gn_tmp = P.alloc(F32, [NT])
                fmKB1 = gn_tmp.bitcast(BF16).rearrange("p (a b) -> p a b", a=2)
                fmKBd = [fmKB0, fmKB1]
                fmKB = fmKBd[0]

                class FMView:
                    def __init__(self, kr, kb):
                        self.kr = kr
                        self.kb = kb

                    def __getitem__(self, key):
                        p_, slot_, c_ = key
                        return (self.kr if slot_ < 2 else self.kb)[p_, slot_ % 2, c_]
                FMV = [FMView(fmKR[0], fmKBd[0]), FMView(fmKR[1], fmKBd[1])]
                ktm_d = [P.alloc(BF16, [NCH, 64]) for _ in range(2)]; nbtm_d = [P.alloc(BF16, [NCH, 64]) for _ in range(2)]
                st0_d = [P.alloc(BF16, [NCH, 3, 64]) for _ in range(2)]
                YX1 = P.alloc(BF16, [NCH, 2, 64])
                YX = [YX1, YX1]
                NTf = P.alloc(F32, [NCH, 64]); NTb_d = [P.alloc(BF16, [NCH, 64]) for _ in range(2)]
                gam_d = [P.alloc(F32, [NCH]) for _ in range(2)]
                NCHAIN = 2 * nseq
                A0 = P.alloc(F32, [NCHAIN, 64]); A0g = P.alloc(F32, [NCHAIN, 64]); A0b = P.alloc(BF16, [NCHAIN, 64])
                rhsb = P.alloc(BF16, [8, 64]); pb = P.alloc(BF16, [8, 64])
                Tn = [zpad[:, j, 0:NT] for j in range(3)] + T456
                TR = ["T1", "T2", "T3", "T4", "T5", "T6"]
                zp4 = zpad.rearrange("p j (s t) -> p j s t", s=nseq)
                for wb_ in range(2):
                    P.memset("pool", wsm2[wb_], 0.0, writes=[("wsm", wb_)])

                def load_hp_weights(hp_):
                    c0_ = hp_ * 128
                    wsm_ = wsm2[hp_ % 2]
                    wr_ = ("wsm", hp_ % 2)
                    for j in range(3):
                        col = OFF_RKV + j * 2048 + c0_
                        P.dma("pool", wrk[j], w_in[:, col:col + 128].rearrange("(kc p) n -> p kc n", p=128), writes=[("wrk", j)])
                    for d in range(2):
                        P.dma("pool", wsm_[64 * d:64 * d + 64, d, :], w_up[d, :, c0_:c0_ + 128], writes=[wr_])
                        P.dma("pool", wsm_[64 * d:64 * d + 64, 2 + d, :], a_up[d, :, c0_:c0_ + 128], writes=[wr_])
                    P.dma("pool", wsm_[:, 4, :], g_up[:, c0_:c0_ + 128], writes=[wr_])
                load_hp_weights(0)
                FM_R = ["fm0", "fm1", "fm2", "fm3"]

                def hd(out_fn, lhs_fn, rhs_fn, st, sp):
                    return [(out_fn(e), lhs_fn(e), rhs_fn(e), st, sp, (64 * e, 64 * e)) for e in range(2)]

                def sl(e):
                    return slice(64 * e, 64 * e + 64)

                def hp_ctx(hp):
                    c0 = hp * 128
                    wsm = wsm2[hp % 2]
                    wsmR = ("wsm", hp % 2)
                    r_, k_, v_ = rkv[:, 0, :], rkv[:, 1, :], rkv[:, 2, :]
                    vtm = vtm_b[hp % 2]
                    vtmR = ("vtm", hp % 2)

                    def gen_z():
                        for j in range(3):
                            P.memset("pool", zp4[:, j, :, 0:1], 0.0, writes=[TR[j]])
                            yield
                            P.memset("pool", zp4[:, j, :, T + 1:T + 2], 0.0, writes=[TR[j]])
                            yield
                            wbuf = wrk[j]
                            wreg = ("wrk", j)
                            for half in range(2):
                                ps = psb[half]
                                P.pe([(ps, wbuf[:, kc, :], hT[:, kc, half * 512:(half + 1) * 512], kc == 0, kc == KC - 1, None)
                                      for kc in range(KC)], reads=[wreg] + HT_R, writes=[PSR(half)])
                                yield
                                spc = 512 // T if T < 512 else 1
                                if T >= 512:
                                    P.copy("act", zp4[:, j, 0, 1 + half * 512:1 + (half + 1) * 512], ps, reads=[PSR(half)], writes=[TR[j]])
                                    yield
                                else:
                                    P.copy("act", zp4[:, j, half * spc:(half + 1) * spc, 1:T + 1],
                                           ps.rearrange("p (s t) -> p s t", s=spc), reads=[PSR(half)], writes=[TR[j]])
                                    yield
                            o3 = rkv[:, j, :].rearrange("p (s t) -> p s t", s=nseq)
                            cc = j * 16 + hp
                            P.ts("dve", o3, zp4[:, j, :, 1:T + 1], cv["conv1"][:, cc:cc + 1], ALU.mult, reads=[TR[j], "cv_conv1"], writes=[("rkv", j)])
                            yield
                            P.stt("dve", o3, zp4[:, j, :, 0:T], cv["conv0"][:, cc:cc + 1], o3, ALU.mult, ALU.add,
                                  reads=[TR[j], "cv_conv0", ("rkv", j)], writes=[("rkv", j)])
                            yield
                            P.stt("dve", o3, zp4[:, j, :, 2:T + 2], cv["conv2"][:, cc:cc + 1], o3, ALU.mult, ALU.add,
                                  reads=[TR[j], "cv_conv2", ("rkv", j)], writes=[("rkv", j)])
                            yield
                        P.ts("dve", kk, k_, cv["k_k"][:, hp:hp + 1], ALU.mult, reads=[("rkv", 1), "cv_k_k"], writes=["kk"])
                        yield
                        P.tt("pool", Tn[0], kk, kk, ALU.mult, reads=["kk"], writes=["T1"])
                        yield
                        for half in range(2):
                            hs = slice(half * 512, (half + 1) * 512)
                            P.pe([(psb[half], bones, Tn[0][:, hs], True, True, None)], reads=["T1", "ct"], writes=[PSR(half)])
                            yield
                            P.act(Tn[1][:, hs], psb[half], AF.Ln, reads=[PSR(half), "ct"], writes=["T2"], bias=eps12, scale=1.0)
                            yield
                        P.act(Tn[1], Tn[1], AF.Exp, reads=["T2"], writes=["T2"], scale=-0.5)
                        yield
                        P.tt("dve", kk, kk, Tn[1], ALU.mult, reads=["kk", "T2"], writes=["kk"])
                        yield
                        P.copy("act", fmKB[:, 0, :], v_, reads=[("rkv", 2)], writes=["fm2_0"])
                        yield
                        for half in range(2):
                            P.pe([mm for c in range(8 * half, 8 * half + 8) for mm in
                                  hd(lambda e, c=c: psb[half][sl(e), (c % 8) * 64:(c % 8) * 64 + 64],
                                     lambda e, c=c: fmKB[sl(e), 0, c * 64:(c + 1) * 64],
                                     lambda e: ident[sl(e), sl(e)], True, True)],
                                 reads=["fm2_0", "ident"], writes=[PSR(half)])
                            yield
                            P.copy("act", vtm[:, 8 * half:8 * half + 8, :], psb[half].rearrange("p (c v) -> p c v", c=8),
                                   reads=[PSR(half)], writes=[vtmR])
                            yield

                    def dirs():
                        def gen_prep(d):
                            fmT = FMV[d]; ktm = ktm_d[d]; nbtm = nbtm_d[d]; st0 = st0_d[d]; NTb = NTb_d[d]; gam = gam_d[d]
                            fm0R = f"fm0_{d}"; fm1R = f"fm1_{d}"; gamR = f"gam{d}"; ktmR = f"ktm{d}"; nbtmR = f"nbtm{d}"
                            fm2R = f"fm2_{d}"; fm3R = f"fm3_{d}"
                            FM_R = [fm0R, fm1R, fm2R, fm3R]
                            kdT = Tn[4 + d]
                            kdR = TR[4 + d]
                            for half in range(2):
                                hs = slice(half * 512, (half + 1) * 512)
                                P.pe([(psb[half], wsm[:, 2 + d, :], ad_m[:, hs], True, True, None)], reads=[wsmR, "ad_m"], writes=[PSR(half)])
                                yield
                                P.act(Tn[0][:, hs], psb[half], AF.Sigmoid, reads=[PSR(half), "cv_a0"], writes=["T1"],
                                      bias=cv["a0"][:, d * 16 + hp:d * 16 + hp + 1], scale=1.0)
                                yield
                            for half in range(2):
                                hs = slice(half * 512, (half + 1) * 512)
                                P.pe([(psb[half], wsm[:, d, :], twd_m[:, hs], True, True, None)], reads=[wsmR, "twd_m"], writes=[PSR(half)])
                                yield
                                P.act(Tn[1][:, hs], psb[half], AF.Sigmoid, reads=[PSR(half), "cv_w0"], writes=["T2"],
                                      bias=cv["w0"][:, d * 16 + hp:d * 16 + hp + 1], scale=1.0)
                                yield
                            a_, lw_, ci_, b_ = Tn[0], Tn[1], Tn[2], Tn[3]
                            P.ts("dve", lw_, lw_, float(-np.exp(-0.5)), ALU.mult, reads=["T2"], writes=["T2"])
                            yield
                            P.tt("pool", b_, kk, a_, ALU.mult, reads=["kk", "T1"], writes=["T4"])
                            yield
                            P.ts("dve", kdT, a_, cv["k_a"][:, hp:hp + 1], ALU.mult, s2=omka[:, hp:hp + 1], op1=ALU.add,
                                 reads=["T1", "cv_k_a", "omka"], writes=[kdR])
                            yield
                            P.tt("dve", kdT, kdT, k_, ALU.mult, reads=[kdR, ("rkv", 1)], writes=[kdR])
                            yield
                            S.op("dve", lambda e, ci_=ci_, lw_=lw_: e.tensor_tensor_scan(out=ci_, data0=scanmask, data1=lw_, initial=0.0,
                                                                                      op0=ALU.mult, op1=ALU.add),
                                 reads=["T2", "ct"], writes=["T3"])
                            yield
                            ci3 = ci_.rearrange("p (c t) -> p c t", t=64)
                            lw3 = lw_.rearrange("p (c t) -> p c t", t=64)
                            if d == 0:
                                P.copy("dve", gam, ci3[:, :, 63], reads=["T3"], writes=[gamR])
                                yield
                            else:
                                P.copy("dve", gam, ci3[:, :, 63], reads=["T3"], writes=[gamR])
                                yield
                                P.tt("dve", ci_, lw_, ci_, ALU.subtract, reads=["T2", "T3"], writes=["T3"])
                                yield
                                P.tt("dve", ci3, ci3, gam.unsqueeze(2).to_broadcast([128, NCH, 64]), ALU.add, reads=["T3", gamR], writes=["T3"])
                                yield
                            ce_ = Tn[0]
                            P.tt("dve", ce_, ci_, lw_, ALU.subtract, reads=["T3", "T2", "T4", kdR], writes=["T1"])
                            yield
                            P.act(ce_, ce_, AF.Exp, reads=["T1"], writes=["T1"])
                            yield
                            P.tt("dve", fmT[:, 0, :], kk, ce_, ALU.mult, reads=["kk", "T1", vtmR], writes=[fm0R])
                            yield
                            P.act(lw_, ci_, AF.Exp, reads=["T3", "T1"], writes=["T2"])
                            yield
                            P.tt("pool", fmT[:, 1, :], r_, lw_, ALU.mult, reads=[("rkv", 0), "T2"], writes=[fm1R])
                            yield
                            P.act(ce_, ci_, AF.Exp, reads=["T3", fm0R], writes=["T1"], scale=-1.0)
                            yield
                            P.tt("dve", fmT[:, 2, :], kdT, ce_, ALU.mult, reads=[kdR, "T1", vtmR], writes=[fm2R])
                            yield
                            P.tt("pool", fmT[:, 3, :], b_, ce_, ALU.mult, reads=["T4", "T1"], writes=[fm3R])
                            yield
                            P.act(gam, gam, AF.Exp, reads=[gamR], writes=[gamR])
                            yield
                            for (src, dst, dreg, sc) in ((2, ktm, ktmR, 1.0), (3, nbtm, nbtmR, -1.0)):
                                for half in range(2):
                                    P.pe([mm for c in range(8 * half, 8 * half + 8) for mm in
                                          hd(lambda e, c=c: psb[half][sl(e), (c % 8) * 64:(c % 8) * 64 + 64],
                                             lambda e, c=c, src=src: fmT[sl(e), src, c * 64:(c + 1) * 64],
                                             lambda e: ident[sl(e), sl(e)], True, True)],
                                         reads=[FM_R[src], "ident"], writes=[PSR(half)])
                                    yield
                                    P.act(dst[:, 8 * half:8 * half + 8, :], psb[half].rearrange("p (c v) -> p c v", c=8), AF.Copy,
                                          reads=[PSR(half)], writes=[dreg], scale=sc)
                                    yield

                        def gen_inv(d):
                            fmT = FMV[d]; ktm = ktm_d[d]; nbtm = nbtm_d[d]; st0 = st0_d[d]; NTb = NTb_d[d]; gam = gam_d[d]
                            fm0R = f"fm0_{d}"; fm1R = f"fm1_{d}"; gamR = f"gam{d}"; ktmR = f"ktm{d}"; nbtmR = f"nbtm{d}"
                            fm2R = f"fm2_{d}"; fm3R = f"fm3_{d}"
                            FM_R = [fm0R, fm1R, fm2R, fm3R]
                            YXa, YXb = YX[0], YX[1]
                            YXa, YXb = YX[0], YX[1]
                            for c in range(NCH):
                                bank = 2 + (c % 4)
                                ps = psb[bank]
                                cs = slice(c * 64, (c + 1) * 64)
                                ops = ((0, 3), (3, 0), (2, 0), (2, 1), (3, 1))
                                P.pe([mm for bi, (li, ri) in enumerate(ops) for mm in
                                      hd(lambda e, bi=bi: ps[sl(e), bi * 64:(bi + 1) * 64],
                                         lambda e, li=li: fmT[sl(e), li, cs],
                                         lambda e, ri=ri: fmT[sl(e), ri, cs], True, True)],
                                     reads=FM_R, writes=[PSR(bank)])
                                yield
                                P.tt("dve", YXa[:, c, :, :], ps[:, 0:128].rearrange("p (a b) -> p a b", a=2),
                                     maskd[d][:, 0:128].rearrange("p (a b) -> p a b", a=2), ALU.mult,
                                     reads=[PSR(bank), "ct"], writes=[("YX0", c)])
                                yield
                                P.tt("dve", st0[:, c, :, :], ps[:, 128:320].rearrange("p (a b) -> p a b", a=3),
                                     maskd[d][:, 128:320].rearrange("p (a b) -> p a b", a=3), ALU.mult,
                                     reads=[PSR(bank), "ct"], writes=[("st0", d, c)])
                                yield
                            YXR = [[("YX0", c) for c in range(NCH)], [("YX0", c) for c in range(NCH)]]
                            for hh in range(2):
                                c8 = slice(8 * hh, 8 * hh + 8)
                                P.tt("dve", NTf[:, c8, :], YXa[:, c8, 1, :], I2.unsqueeze(1).to_broadcast([128, 8, 64]), ALU.add,
                                     reads=YXR[0][8 * hh:8 * hh + 8] + ["I2"], writes=[("NTf", hh)])
                                yield
                            for hh in range(2):
                                c8 = slice(8 * hh, 8 * hh + 8)
                                P.copy("act", NTb[:, c8, :], NTf[:, c8, :], reads=[("NTf", hh)], writes=[("NTb", d, hh)])
                                yield
                            cur = 0
                            for lvl in range(1, 6):
                                old, new = YX[cur], YX[1 - cur]
                                oldR, newR = YXR[cur], YXR[1 - cur]
                                for hh in range(2):
                                    mms = []
                                    for c in range(8 * hh, 8 * hh + 8):
                                        pso = psb[2 + 2 * hh + (c % 8) // 4][:, (c % 4) * 128:(c % 4) * 128 + 128]
                                        mms += hd(lambda e, pso=pso: pso[sl(e), 0:64], lambda e, c=c: old[sl(e), c, 1, :], lambda e, c=c: old[sl(e), c, 0, :], True, True)
                                        if lvl < 5:
                                            mms += hd(lambda e, pso=pso: pso[sl(e), 64:128], lambda e, c=c: old[sl(e), c, 0, :], lambda e, c=c: old[sl(e), c, 1, :], True, True)
                                    P.pe(mms, reads=oldR[8 * hh:8 * hh + 8], writes=[PSR(2 + 2 * hh), PSR(3 + 2 * hh)])
                                    yield
                                for hh in range(2):
                                    for bk in range(2):
                                        c_lo = 8 * hh + 4 * bk
                                        src4 = psb[2 + 2 * hh + bk].rearrange("p (c a b) -> p c a b", c=4, a=2)
                                        eng_ = "act" if bk == 0 else "dve"
                                        if lvl < 5:
                                            P.copy(eng_, new[:, c_lo:c_lo + 4, :, :], src4,
                                                   reads=[PSR(2 + 2 * hh + bk)], writes=newR[c_lo:c_lo + 4])
                                            yield
                                        else:
                                            P.copy(eng_, new[:, c_lo:c_lo + 4, 0, :], src4[:, :, 0, :],
                                                   reads=[PSR(2 + 2 * hh + bk)], writes=newR[c_lo:c_lo + 4])
                                            yield
                                for hh in range(2):
                                    mms = []
                                    for c in range(8 * hh, 8 * hh + 8):
                                        mms += hd(lambda e, c=c: psb[6 + hh][sl(e), (c % 8) * 64:(c % 8) * 64 + 64], lambda e, c=c: new[sl(e), c, 0, :],
                                                  lambda e, c=c: NTb[sl(e), c, :], True, True)
                                    P.pe(mms, reads=newR[8 * hh:8 * hh + 8] + [("NTb", d, hh)], writes=[PSR(6 + hh)])
                                    yield
                                for hh in range(2):
                                    c8 = slice(8 * hh, 8 * hh + 8)
                                    P.tt("dve", NTf[:, c8, :], NTf[:, c8, :], psb[6 + hh].rearrange("p (c t) -> p c t", c=8), ALU.add,
                                         reads=[("NTf", hh), PSR(6 + hh)], writes=[("NTf", hh)])
                                    yield
                                for hh in range(2):
                                    c8 = slice(8 * hh, 8 * hh + 8)
                                    P.copy("act", NTb[:, c8, :], NTf[:, c8, :], reads=[("NTf", hh)], writes=[("NTb", d, hh)])
                                    yield
                                cur = 1 - cur

                        def interleave2(ga, gb):
                            gens = [g for g in (ga, gb) if g is not None]
                            while gens:
                                for g in list(gens):
                                    try:
                                        next(g)
                                    except StopIteration:
                                        gens.remove(g)
                        interleave2(gen_prep(0), None)
                        interleave2(gen_inv(0), gen_prep(1))
                        interleave2(gen_inv(1), None)
                        t4_ = Tn[3]
                        P.tt("pool", t4_, Tn[4], Tn[5], ALU.add, reads=["T5", "T6"], writes=["T4"])
                        P.stt("dve", t4_, r_, cv["r_k"][:, hp:hp + 1], t4_, ALU.mult, ALU.mult, reads=[("rkv", 0), "cv_r_k", "T4"], writes=["T4"])
                        for half in range(2):
                            hs = slice(half * 512, (half + 1) * 512)
                            P.pe([(psb[half], bones, t4_[:, hs], True, True, None)], reads=["T4", "ct"], writes=[PSR(half)])
                            P.tt("dve", t4_[:, hs], v_[:, hs], psb[half], ALU.mult, reads=[("rkv", 2), PSR(half), "T4"], writes=["T4"])

                    def gen_chains():
                        chains = []
                        for d in range(2):
                            for s_ in range(nseq):
                                cl = list(range(s_ * CPS, (s_ + 1) * CPS))
                                if d == 1:
                                    cl = cl[::-1]
                                chains.append((d, s_, d * nseq + s_, cl))
                        for (d, s_, ch, cl) in chains:
                            a0r = ("A0", ch)
                            if has_cache:
                                P.dma("sp", A0g[:, ch, :], st_in[d, 2 * hp:2 * hp + 2].rearrange("e v k -> (e v) k"), writes=[("A0g", ch)])
                                yield
                                P.pe(hd(lambda e: psb[4][sl(e), 0:64], lambda e: A0g[sl(e), ch, :], lambda e: identf[sl(e), sl(e)], True, True),
                                     reads=[("A0g", ch), "ct"], writes=[PSR(4)])
                                yield
                                P.copy("dve", A0[:, ch, :], psb[4][:, 0:64], reads=[PSR(4)], writes=[a0r])
                                yield
                            else:
                                P.memset("pool", A0[:, ch, :], 0.0, writes=[a0r])
                                yield
                            P.copy("act", A0b[:, ch, :], A0[:, ch, :], reads=[a0r], writes=[("A0b", ch)])
                            yield
                        for step in range(CPS):
                            info = []
                            for (d, s_, ch, cl) in chains:
                                c = cl[step]
                                if len(chains) <= 6:
                                    slot = 2 + ch
                                    bkc = psb[slot]
                                elif ch < 4:
                                    slot = 2 + ch
                                    bkc = psb[slot]
                                else:
                                    slot = 6 + (ch - 4) // 2
                                    bkc = psb[slot][:, (ch % 2) * 256:(ch % 2) * 256 + 256]
                                info.append((d, ch, c, slice(c * 64, (c + 1) * 64), slot, bkc[:, 0:64], bkc[:, 64:128], bkc[:, 128:192], bkc[:, 192:256],
                                             ("A0", ch), ("A0b", ch)))
                            for (d, ch, c, cs, slot, psR, psP, psY, psS, a0r, a0br) in info:
                                P.ts("dve", A0g[:, ch, :], A0[:, ch, :], gam_d[d][:, c:c + 1], ALU.mult, reads=[a0r, f"gam{d}"], writes=[("A0g", ch)])
                                yield
                                P.pe(hd(lambda e: psR[sl(e), :], lambda e: fmKR[d][sl(e), 0, cs], lambda e: A0b[sl(e), ch, :], True, False) +
                                     hd(lambda e: psR[sl(e), :], lambda e: st0_d[d][sl(e), c, 0, :], lambda e: vtm[sl(e), c, :], False, True),
                                     reads=[f"fm0_{d}", a0br, ("st0", d, c), vtmR], writes=[PSR(slot)])
                                yield
                            for (d, ch, c, cs, slot, psR, psP, psY, psS, a0r, a0br) in info:
                                P.copy("act", rhsb[:, ch, :], psR, reads=[PSR(slot)], writes=[("rhsb", ch)])
                                yield
                            for (d, ch, c, cs, slot, psR, psP, psY, psS, a0r, a0br) in info:
                                P.pe(hd(lambda e: psP[sl(e), :], lambda e: NTb_d[d][sl(e), c, :], lambda e: rhsb[sl(e), ch, :], True, True),
                                     reads=[("NTb", d, c // 8), ("rhsb", ch)], writes=[PSR(slot)])
                                yield
                            for (d, ch, c, cs, slot, psR, psP, psY, psS, a0r, a0br) in info:
                                P.copy("dve", pb[:, ch, :], psP, reads=[PSR(slot)], writes=[("pb", ch)])
                                yield
                            for (d, ch, c, cs, slot, psR, psP, psY, psS, a0r, a0br) in info:
                                P.pe(hd(lambda e: psY[sl(e), :], lambda e: A0b[sl(e), ch, :], lambda e: fmKR[d][sl(e), 1, cs], True, False) +
                                     hd(lambda e: psY[sl(e), :], lambda e: vtm[sl(e), c, :], lambda e: st0_d[d][sl(e), c, 1, :], False, False) +
                                     hd(lambda e: psY[sl(e), :], lambda e: pb[sl(e), ch, :], lambda e: st0_d[d][sl(e), c, 2, :], False, True) +
                                     hd(lambda e: psS[sl(e), :], lambda e: ktm_d[d][sl(e), c, :], lambda e: vtm[sl(e), c, :], True, False) +
                                     hd(lambda e: psS[sl(e), :], lambda e: nbtm_d[d][sl(e), c, :], lambda e: pb[sl(e), ch, :], False, True),
                                     reads=[a0br, f"fm1_{d}", vtmR, ("st0", d, c), ("pb", ch), f"ktm{d}", f"nbtm{d}"], writes=[PSR(slot)])
                                yield
                            for (d, ch, c, cs, slot, psR, psP, psY, psS, a0r, a0br) in info:
                                P.stt("dve", A0[:, ch, :], psS, gam_d[d][:, c:c + 1], A0g[:, ch, :], ALU.mult, ALU.add,
                                      reads=[PSR(slot), f"gam{d}", ("A0g", ch)], writes=[a0r])
                                yield
                            for (d, ch, c, cs, slot, psR, psP, psY, psS, a0r, a0br) in info:
                                P.copy("act", A0b[:, ch, :], A0[:, ch, :], reads=[a0r], writes=[a0br])
                                yield
                            for (d, ch, c, cs, slot, psR, psP, psY, psS, a0r, a0br) in info:
                                P.tt("dve", ysum[:, cs], ysum[:, cs], psY, ALU.add, reads=[PSR(slot), ("ysum", c), "ysum"], writes=[("ysum", c)])
                                yield
                        if gi == 0:
                            for (d, s_, ch, cl) in chains:
                                P.pe(hd(lambda e: psb[2 + ch % 6][sl(e), 0:64], lambda e: A0[sl(e), ch, :],
                                        lambda e: identf[sl(e), sl(e)], True, True),
                                     reads=[("A0", ch), "ct"], writes=[PSR(2 + ch % 6)])
                                yield
                                P.copy("dve", A0g[:, ch, :], psb[2 + ch % 6][:, 0:64],
                                       reads=[PSR(2 + ch % 6)], writes=[("A0g", ch)])
                                yield
                                P.dma("sp", nst[s_, d, 2 * hp:2 * hp + 2].rearrange("e v k -> (e v) k"), A0g[:, ch, :],
                                      reads=[("A0g", ch)])
                                yield

                    def groupnorm():
                        YS_R = [("ysum", c) for c in range(NCH)] + ["ysum"]
                        mean_, d_, t3_ = Tn[4], Tn[5], gn_tmp
                        GT_R = ["fm2_1", "fm3_1"]
                        for half in range(2):
                            hs = slice(half * 512, (half + 1) * 512)
                            P.pe([(psb[half], bavg, ysum[:, hs], True, True, None)], reads=YS_R + ["ct"], writes=[PSR(half)])
                            P.tt("dve", d_[:, hs], ysum[:, hs], psb[half], ALU.subtract, reads=YS_R + [PSR(half)], writes=["T6"])
                        P.tt("pool", t3_, d_, d_, ALU.mult, reads=["T6"], writes=GT_R)
                        for half in range(2):
                            hs = slice(half * 512, (half + 1) * 512)
                            P.pe([(psb[half], bavg, t3_[:, hs], True, True, None)], reads=GT_R + ["ct"], writes=[PSR(half)])
                            P.act(mean_[:, hs], psb[half], AF.Ln, reads=[PSR(half), "ct"], writes=["T5"], bias=epslnx, scale=1.0)
                        P.act(mean_, mean_, AF.Exp, reads=["T5"], writes=["T5"], scale=-0.5)
                        P.tt("dve", d_, d_, mean_, ALU.mult, reads=["T6", "T5"], writes=["T6"])
                        P.ts("dve", d_, d_, cv["lnx_w"][:, hp:hp + 1], ALU.mult, s2=cv["lnx_b"][:, hp:hp + 1], op1=ALU.add,
                             reads=["T6", "cv_lnx_w", "cv_lnx_b"], writes=["T6"])
                        P.tt("dve", d_, d_, Tn[3], ALU.add, reads=["T6", "T4"], writes=["T6"])
                        for half in range(2):
                            hs = slice(half * 512, (half + 1) * 512)
                            P.pe([(psb[half], wsm[:, 4, :], sgdT[:, hs], True, True, None)], reads=[wsmR, "sgdT"], writes=[PSR(half)])
                            P.tt("dve", orwkvT[:, hp, hs], d_[:, hs], psb[half], ALU.mult, reads=["T6", PSR(half)], writes=[("orwkvT", hp)])
                        P.memset("pool", ysum, 0.0, writes=["ysum"])

                    return gen_z, dirs, gen_chains, groupnorm

                def interleaveN(ga, gb):
                    gens = [g for g in (ga, gb) if g is not None]
                    while gens:
                        for g in list(gens):
                            try:
                                next(g)
                            except StopIteration:
                                gens.remove(g)
                ctxs = [hp_ctx(hp) for hp in range(16)]
                P.memset("pool", ysum, 0.0, writes=["ysum"])
                interleaveN(ctxs[0][0](), None)
                load_hp_weights(1)
                for hp in range(16):
                    ctxs[hp][1]()
                    interleaveN(ctxs[hp][2](), ctxs[hp + 1][0]() if hp + 1 < 16 else None)
                    ctxs[hp][3]()
                    if hp + 2 < 16:
                        load_hp_weights(hp + 2)
                P.release(mR)

            phase_R()
            OR_R = [("orwkvT", hp) for hp in range(16)]
            P.dbg(f"orwkvT{gi}", orwkvT, [16, NT], OR_R)
            if stop_after == "R":
                return

            omlaT = P.alloc(BF16, [16, NT])
            m_kv = P.mark()
            NKT = 10 if has_cache else 8
            ckvnT = P.alloc(BF16, [4, NKT * 128])
            krgT = P.alloc(F32, [NKT * 128])
            sqKR = P.alloc(BF16, [NKT * 128])
            mB = P.mark()
            wck = P.alloc(BF16, [KC, 576])
            P.dma("pool", wck, w_in[:, OFF_CKV:OFF_CKV + 576].rearrange("(kc p) n -> p kc n", p=128), writes=["wck"])
            zc = [P.alloc(F32, [576]) for _ in range(2)]
            cn = [P.alloc(BF16, [512]) for _ in range(2)]
            sq = P.alloc(F32, [2 * NKT])
            junkb = P.alloc(BF16, [512])
            tmpr = P.alloc(F32, [2, 64])
            for kt in range(NKT):
                zt = zc[kt % 2]
                zr = ("zc", kt % 2)
                if kt < 8:
                    P.pe([(psb[2][:, 0:512], hT[:, kc, kt * 128:(kt + 1) * 128], wck[:, kc, 0:512], kc == 0, kc == KC - 1, None)
                          for kc in range(KC)] +
                         [(psb[3][:, 0:64], hT[:, kc, kt * 128:(kt + 1) * 128], wck[:, kc, 512:576], kc == 0, kc == KC - 1, None)
                          for kc in range(KC)], reads=["wck", ("hT", kt)], writes=[PSR(2), PSR(3)])
                    P.copy("act", zt[:, 0:512], psb[2][:, 0:512], reads=[PSR(2)], writes=[zr])
                    P.copy("dve", zt[:, 512:576], psb[3][:, 0:64], reads=[PSR(3)], writes=[zr])
                    if gi == 0:
                        P.dma("sp", nckv[kt * 128:(kt + 1) * 128, :], zt[:, 0:512], reads=[zr])
                        P.dma("sp", nkr[kt * 128:(kt + 1) * 128, :], zt[:, 512:576], reads=[zr])
                else:
                    P.dma("sp", zt[:, 0:512], cckv[(kt - 8) * 128:(kt - 7) * 128, :], writes=[zr])
                    P.dma("sp", zt[:, 512:576], ckr[(kt - 8) * 128:(kt - 7) * 128, :], writes=[zr])
                P.act(junkb, zt[:, 0:512], AF.Square, reads=[zr], writes=[("sqB", kt)], accum_out=sq[:, kt:kt + 1])
                P.act(sq[:, NKT + kt:NKT + kt + 1], sq[:, kt:kt + 1], AF.Sqrt, reads=[("sqB", kt), "ct"], writes=[("sqB2", kt)],
                      bias=eps1, scale=float(1.0 / 512.0))
                P.recip(sq[:, NKT + kt:NKT + kt + 1], sq[:, NKT + kt:NKT + kt + 1], reads=[("sqB2", kt)], writes=[("sqB2", kt)])
                cb = cn[kt % 2]
                P.ts("dve", cb, zt[:, 0:512], sq[:, NKT + kt:NKT + kt + 1], ALU.mult, reads=[zr, ("sqB2", kt)], writes=[("cn", kt % 2)])
                pst = psb[4]
                P.pe([(pst[:, j * 128:(j + 1) * 128], cb[:, j * 128:(j + 1) * 128], ident, True, True, None) for j in range(4)],
                     reads=[("cn", kt % 2), "ident"], writes=[PSR(4)])
                for j in range(4):
                    P.act(ckvnT[:, j, kt * 128:(kt + 1) * 128], pst[:, j * 128:(j + 1) * 128], AF.Copy,
                          reads=[PSR(4), "cv_kv_norm"], writes=[("ckvnT", kt)], scale=cv["kv_norm"][:, j:j + 1])
            krT = P.alloc(F32, [512])
            rtm_ = P.alloc(F32, [512])
            NKH = (NKT * 128 + 511) // 512
            for kh in range(NKH):
                k0 = kh * 512
                n_ = min(512, NKT * 128 - k0)
                ks = slice(k0, k0 + n_)
                if kh < 2:
                    P.pe([(psb[5][0:64, 0:512], wck[:, kc, 512:576], hT[:, kc, ks], kc == 0, kc == KC - 1, None) for kc in range(KC)],
                         reads=["wck"] + HT_R, writes=[PSR(5)])
                else:
                    P.pe([(psb[5][0:64, (kt - 8) * 128:(kt - 7) * 128], zc[kt % 2][:, 512:576], identf, True, True, None)
                          for kt in range(8, NKT)], reads=[("zc", 0), ("zc", 1), "ct"], writes=[PSR(5)])
                P.act(sqKR[0:64, ks], psb[5][0:64, 0:n_], AF.Square, reads=[PSR(5)], writes=["sqKR"])
                if has_cache and kh < 2:
                    P.act(krT[0:64, 0:n_], psb[5][0:64, 0:n_], AF.Copy, reads=[PSR(5), "gcol"], writes=["krT"], scale=gcol[0:64, 3:4])
                    P.pe([(psb[6][0:64, 0:n_], permf, krT[0:64, 0:n_], True, True, None)], reads=["krT", "ct"], writes=[PSR(6)])
                    P.tt("dve", rtm_[0:64, 0:n_], psb[6][0:64, 0:n_], ropeST[:, ks], ALU.mult, reads=[PSR(6), "ct"], writes=["rtm_"])
                    P.tt("dve", krT[0:64, 0:n_], krT[0:64, 0:n_], ropeCT[:, ks], ALU.mult, reads=["krT", "ct"], writes=["krT"])
                    P.tt("dve", krgT[0:64, ks], krT[0:64, 0:n_], rtm_[0:64, 0:n_], ALU.add, reads=["krT", "rtm_"], writes=["krgT"])
                else:
                    P.act(krgT[0:64, ks], psb[5][0:64, 0:n_], AF.Copy, reads=[PSR(5), "gcol"], writes=["krgT"], scale=gcol[0:64, 3:4])
            P.release(mB)
            P.dbg(f"ckvnT{gi}", ckvnT, [4, NKT * 128], [("ckvnT", kt) for kt in range(NKT)])
            P.dbg(f"krgT{gi}", krgT, [NKT * 128], ["krgT"])
            if stop_after == "B":
                return


            def phase_T():
                mT = P.mark()
                NK = NKT * 128
                wq = [P.alloc(BF16, [KC, 192]) for _ in range(2)]
                wkv = [P.alloc(BF16, [4, 256]) for _ in range(2)]
                qT = [P.alloc(BF16, [NT]) for _ in range(2)]; qTr = [P.alloc(BF16, [NT]) for _ in range(2)]
                kT = [P.alloc(BF16, [NK]) for _ in range(2)]; kTr = [P.alloc(BF16, [NK]) for _ in range(2)]
                Vt = [P.alloc(BF16, [NKT, 128]) for _ in range(2)]
                sqA = P.alloc(BF16, [512]); sqB = P.alloc(BF16, [512])
                rs = P.alloc(F32, [512])
                xg = P.alloc(F32, [512]); xr = P.alloc(F32, [512])
                pT = [P.alloc(BF16, [512]) for _ in range(2)]
                rl = P.alloc(F32, [512])

                def prep(h):
                    hb = h % 2
                    wqb = wq[hb]; wkb = wkv[hb]
                    P.dma("pool", wqb, w_in[:, OFF_Q + h * 192:OFF_Q + (h + 1) * 192].rearrange("(kc p) n -> p kc n", p=128),
                          writes=[("wq", hb)])
                    P.dma("pool", wkb, w_kv_up[:, h * 256:(h + 1) * 256].rearrange("(j p) n -> p j n", p=128), writes=[("wkv", hb)])
                    yield
                    for half in range(2):
                        hs = slice(half * 512, (half + 1) * 512)
                        P.pe([(psb[0], wqb[:, kc, 0:128], hT[:, kc, hs], kc == 0, kc == KC - 1, None) for kc in range(KC)],
                             reads=[("wq", hb)] + HT_R, writes=[PSR(0)])
                        yield
                        P.pe([(psb[1][0:64, :], wqb[:, kc, 128:192], hT[:, kc, hs], kc == 0, kc == KC - 1, None) for kc in range(KC)],
                             reads=[("wq", hb)] + HT_R, writes=[PSR(1)])
                        P.act(sqA, psb[0], AF.Square, reads=[PSR(0)], writes=["sqA"])
                        yield
                        P.act(sqB[0:64, :], psb[1][0:64, :], AF.Square, reads=[PSR(1)], writes=["sqB"])
                        P.pe([(psb[2], ones_bf, sqA, True, False, None), (psb[2], ones_bf[0:64, :], sqB[0:64, :], False, True, None)],
                             reads=["sqA", "sqB", "ones_bf"], writes=[PSR(2)])
                        yield
                        P.act(rs, psb[2], AF.Ln, reads=[PSR(2), "ct"], writes=["rs"], bias=eps1, scale=float(1.0 / 192.0))
                        P.act(rs, rs, AF.Exp, reads=["rs"], writes=["rs"], scale=-0.5)
                        yield
                        P.stt("dve", qT[hb][:, hs], psb[0], gcol[:, 0:1], rs, ALU.mult, ALU.mult, reads=[PSR(0), "gcol", "rs"], writes=[("qT", hb, half)])
                        if has_cache:
                            P.act(xg[0:64, :], psb[1][0:64, :], AF.Copy, reads=[PSR(1), "gcol"], writes=["xg"], scale=gcol[0:64, 1:2])
                            P.pe([(psb[3][0:64, :], permf, xg[0:64, :], True, True, None)], reads=["xg", "ct"], writes=[PSR(3)])
                            yield
                            P.tt("dve", xr[0:64, :], psb[3][0:64, :], ropeST[:, hs], ALU.mult, reads=[PSR(3), "ct"], writes=["xr"])
                            P.tt("dve", xg[0:64, :], xg[0:64, :], ropeCT[:, hs], ALU.mult, reads=["xg", "ct"], writes=["xg"])
                            yield
                            P.tt("dve", xg[0:64, :], xg[0:64, :], xr[0:64, :], ALU.add, reads=["xg", "xr"], writes=["xg"])
                            P.tt("dve", qTr[hb][0:64, hs], xg[0:64, :], rs[0:64, :], ALU.mult, reads=["xg", "rs"], writes=[("qT", hb, half)])
                        else:
                            P.stt("dve", qTr[hb][0:64, hs], psb[1][0:64, :], gcol[0:64, 1:2], rs[0:64, :], ALU.mult, ALU.mult,
                                  reads=[PSR(1), "gcol", "rs"], writes=[("qT", hb, half)])
                        yield
                    for kh in range((NK + 511) // 512):
                        k0 = kh * 512
                        n_ = min(512, NK - k0)
                        ks = slice(k0, k0 + n_)
                        kt_r = [("ckvnT", kt) for kt in range(k0 // 128, (k0 + n_) // 128)]
                        P.pe([(psb[0][:, 0:n_], wkb[:, j, 0:128], ckvnT[:, j, ks], j == 0, j == 3, None) for j in range(4)],
                             reads=[("wkv", hb)] + kt_r, writes=[PSR(0)])
                        P.act(sqA[:, 0:n_], psb[0][:, 0:n_], AF.Square, reads=[PSR(0)], writes=["sqA"])
                        yield
                        P.pe([(psb[2][:, 0:n_], ones_bf, sqA[:, 0:n_], True, False, None),
                              (psb[2][:, 0:n_], ones_bf[0:64, :], sqKR[0:64, ks], False, True, None)],
                             reads=["sqA", "sqKR", "ones_bf"], writes=[PSR(2)])
                        P.act(rs[:, 0:n_], psb[2][:, 0:n_], AF.Ln, reads=[PSR(2), "ct"], writes=["rs"], bias=eps1, scale=float(1.0 / 192.0))
                        yield
                        P.act(rs[:, 0:n_], rs[:, 0:n_], AF.Exp, reads=["rs"], writes=["rs"], scale=-0.5)
                        P.stt("dve", kT[hb][:, ks], psb[0][:, 0:n_], gcol[:, 2:3], rs[:, 0:n_], ALU.mult, ALU.mult,
                              reads=[PSR(0), "gcol", "rs"], writes=[("kT", hb, kh)])
                        P.tt("dve", kTr[hb][0:64, ks], krgT[0:64, ks], rs[0:64, 0:n_], ALU.mult, reads=["krgT", "rs"], writes=[("kT", hb, kh)])
                        yield
                        for kt in range(k0 // 128, (k0 + n_) // 128):
                            pv = psb[1] if kt % 2 == 0 else psb[3]
                            P.pe([(pv[:, 0:128], ckvnT[:, j, kt * 128:(kt + 1) * 128], wkb[:, j, 128:256], j == 0, j == 3, None) for j in range(4)],
                                 reads=[("wkv", hb), ("ckvnT", kt)], writes=[PSR(1 if kt % 2 == 0 else 3)])
                            P.copy("act", Vt[hb][:, kt, :], pv[:, 0:128], reads=[PSR(1 if kt % 2 == 0 else 3)], writes=[("Vt", hb, kt)])
                            yield

                def attn(h):
                    hb = h % 2
                    if has_cache:
                        blocks = [(qb * 512, 512, list(range(NKT))) for qb in range(2)]
                    else:
                        blocks = [(s_ * 256, 256, [2 * s_, 2 * s_ + 1]) for s_ in range(4)]
                    for bi, (q0, nq, kts) in enumerate(blocks):
                        qs = slice(q0, q0 + nq)
                        q_r = [("qT", hb, q0 // 512)]
                        po = psb[6]; pl = psb[7]
                        for ki, kt in enumerate(kts):
                            pS = psb[4 + ki % 2]
                            P.pe([(pS[:, 0:nq], kT[hb][:, kt * 128:(kt + 1) * 128], qT[hb][:, qs], True, False, None),
                                  (pS[:, 0:nq], kTr[hb][0:64, kt * 128:(kt + 1) * 128], qTr[hb][0:64, qs], False, True, None)],
                                 reads=[("kT", hb, kt // 4)] + q_r, writes=[PSR(4 + ki % 2)])
                            pt_ = pT[ki % 2]
                            P.act(pt_[:, 0:nq], pS[:, 0:nq], AF.Exp, reads=[PSR(4 + ki % 2)], writes=[("pT", ki % 2)])
                            P.pe([(po[:, 0:nq], Vt[hb][:, kt, :], pt_[:, 0:nq], ki == 0, ki == len(kts) - 1, None),
                                  (pl[:, 0:nq], ones_bf, pt_[:, 0:nq], ki == 0, ki == len(kts) - 1, None)],
                                 reads=[("Vt", hb, kt), ("pT", ki % 2), "ones_bf"], writes=[PSR(6), PSR(7)])
                            yield
                        P.act(rl[:, 0:nq], pl[:, 0:nq], AF.Ln, reads=[PSR(7)], writes=["rl"])
                        P.act(rl[:, 0:nq], rl[:, 0:nq], AF.Exp, reads=["rl"], writes=["rl"], scale=-1.0)
                        P.tt("dve", omlaT[:, h, qs], po[:, 0:nq], rl[:, 0:nq], ALU.mult, reads=[PSR(6), "rl"], writes=[("omlaT", h)])
                        yield

                def interleave(ga, gb):
                    gens = [g for g in (ga, gb) if g is not None]
                    while gens:
                        for g in list(gens):
                            try:
                                next(g)
                            except StopIteration:
                                gens.remove(g)

                interleave(prep(0), None)
                for h in range(H_MLA):
                    interleave(attn(h), prep(h + 1) if h + 1 < H_MLA else None)
                P.release(mT)

            phase_T()
            P.release(m_kv)
            OM_R = [("omlaT", h) for h in range(16)]
            P.dbg(f"omlaT{gi}", omlaT, [16, NT], OM_R)
            if stop_after == "T":
                return

            P.limit = ARENA_WORDS - 8192
            assert P.top <= P.limit
            mergedT = P.arena[:, P.limit:P.limit + 8192].bitcast(BF16).rearrange("p (a b) -> p a b", a=16)

            def phase_M():
                mM = P.mark()
                wm = [[P.alloc(BF16, [KC, 128]) for _ in range(4)] for _ in range(2)]
                sg = [P.alloc(F32, [512]) for _ in range(2)]
                it = 0
                for fb in range(16):
                    wset = wm[fb % 2]
                    srcs = (w_in[:, OFF_GATE + fb * 128:OFF_GATE + (fb + 1) * 128],
                            w_in[:, OFF_GATE + 2048 + fb * 128:OFF_GATE + 2048 + (fb + 1) * 128],
                            w_br_mla[:, fb * 128:(fb + 1) * 128], w_br_rwkv[:, fb * 128:(fb + 1) * 128])
                    for wi in range(4):
                        P.dma("pool", wset[wi], srcs[wi].rearrange("(kc p) n -> p kc n", p=128), writes=[("wm", fb % 2, wi)])
                    for half in range(2):
                        hs = slice(half * 512, (half + 1) * 512)
                        pb0 = 4 * (it % 2)
                        acts = (hT, hT, omlaT, orwkvT)
                        areg = (HT_R, HT_R, OM_R, OR_R)
                        for wi in range(4):
                            P.pe([(psb[pb0 + wi], wset[wi][:, kc, :], acts[wi][:, kc, hs], kc == 0, kc == KC - 1, None) for kc in range(KC)],
                                 reads=[("wm", fb % 2, wi)] + areg[wi], writes=[PSR(pb0 + wi)])
                        for wi in range(2):
                            P.act(sg[wi], psb[pb0 + wi], AF.Sigmoid, reads=[PSR(pb0 + wi)], writes=[("sg", wi)])
                            P.tt("dve", sg[wi], psb[pb0 + 2 + wi], sg[wi], ALU.mult, reads=[PSR(pb0 + 2 + wi), ("sg", wi)], writes=[("sg", wi)])
                        P.tt("dve", mergedT[:, fb, hs], sg[0], sg[1], ALU.add, reads=[("sg", 0), ("sg", 1)], writes=[("mergedT", fb)])
                        it += 1
                P.release(mM)

            phase_M()
            MG_R = [("mergedT", fb) for fb in range(16)]
            P.dbg(f"mergedT{gi}", mergedT, [16, NT], MG_R)
            if stop_after == "M":
                return

            def gate_bc(dst, sixth):
                onesf = P.alloc(F32, [128])
                dg = [P.alloc(F32, [128]) for _ in range(2)]
                P.memset("pool", onesf, 1.0, writes=["onesf"])
                for blk in range(16):
                    d_ = dg[blk % 2]
                    P.ts("dve", d_, identf, modT[:, sixth * 16 + blk, cj:cj + 1], ALU.mult, reads=["ct", "modT"], writes=[("dg", blk % 2)])
                    bank = (blk // 4) % 2
                    P.pe([(psb[bank][:, (blk % 4) * 128:(blk % 4) * 128 + 128], onesf, d_, True, True, None)],
                         reads=["onesf", ("dg", blk % 2)], writes=[PSR(bank)])
                    if blk % 4 == 3:
                        cb = blk // 4
                        P.copy("act", dst[:, cb * 512:(cb + 1) * 512], psb[bank], reads=[PSR(bank)], writes=["gatebc"])

            def phase_O():
                mO = P.mark()
                g1bc = P.alloc(F32, [D])
                wo = P.alloc(BF16, [KC, D])
                for cb in range(4):
                    P.dma("pool", wo[:, :, cb * 512:(cb + 1) * 512], w_out[:, cb * 512:(cb + 1) * 512].rearrange("(kc p) n -> p kc n", p=128),
                          writes=[("wo", cb)])
                gate_bc(g1bc, 2)
                xb2 = [P.alloc(F32, [D]) for _ in range(2)]
                x1b = [P.alloc(F32, [D]) for _ in range(2)]

                def x1_tile(ti):
                    xt = xb2[ti % 2]; x1 = x1b[ti % 2]
                    P.dma("sp", xt, xg[ti * 128:(ti + 1) * 128, :], writes=[("xb2", ti % 2)])
                    for cb in range(4):
                        pb_ = 4 + cb
                        P.pe([(psb[pb_], mergedT[:, kc, ti * 128:(ti + 1) * 128], wo[:, kc, cb * 512:(cb + 1) * 512], kc == 0, kc == KC - 1, None)
                              for kc in range(KC)], reads=[("wo", cb)] + MG_R, writes=[PSR(pb_)])
                        cs_ = slice(cb * 512, (cb + 1) * 512)
                        P.tt("dve", x1[:, cs_], psb[pb_], g1bc[:, cs_], ALU.mult, reads=[PSR(pb_), "gatebc"], writes=[("x1b", ti % 2)])
                        P.tt("pool", x1[:, cs_], x1[:, cs_], xt[:, cs_], ALU.add, reads=[("x1b", ti % 2), ("xb2", ti % 2)], writes=[("x1b", ti % 2)])
                    P.dma("sp", yg[ti * 128:(ti + 1) * 128, :], x1, reads=[("x1b", ti % 2)], writes=[("yscr", ti)])
                    return x1, ("x1b", ti % 2)
                norm_to_T(x1_tile, g2[:, cj, :], sh2[:, cj, :], hT, "hT")
                P.release(mO)

            P.release(m_h)
            phase_O()
            P.dbg(f"h2T{gi}", hT, [KC, NT], HT_R)
            if stop_after == "O":
                return

            def phase_F():
                mF = P.mark()
                g2bc = P.alloc(F32, [D])
                acc = P.alloc(F32, [8, D])
                w1 = [P.alloc(BF16, [KC, 512]) for _ in range(2)]
                w2 = P.alloc(BF16, [4, D])
                uT = P.alloc(BF16, [4, NT])
                rtm = [P.alloc(F32, [512]) for _ in range(2)]

                def load_ff(hb_):
                    P.dma("pool", w1[hb_ % 2], w_ff_in[:, hb_ * 512:(hb_ + 1) * 512].rearrange("(kc p) n -> p kc n", p=128), writes=[("w1", hb_ % 2)])
                    P.dma("pool", w2, w_ff_out[hb_ * 512:(hb_ + 1) * 512, :].rearrange("(j p) n -> p j n", p=128), writes=["w2"])
                load_ff(0)
                gate_bc(g2bc, 5)
                for hb in range(16):
                    if hb > 0:
                        load_ff(hb)
                    it = 0
                    for j in range(4):
                        for half in range(2):
                            hs = slice(half * 512, (half + 1) * 512)
                            pb_ = it % 2
                            P.pe([(psb[pb_], w1[hb % 2][:, kc, j * 128:(j + 1) * 128], hT[:, kc, hs], kc == 0, kc == KC - 1, None) for kc in range(KC)],
                                 reads=[("w1", hb % 2)] + HT_R, writes=[PSR(pb_)])
                            P.act(rtm[it % 2], psb[pb_], AF.Relu, reads=[PSR(pb_)], writes=[("rtm", it % 2)])
                            P.act(uT[:, j, hs], rtm[it % 2], AF.Square, reads=[("rtm", it % 2)], writes=[("uT", j, half)])
                            it += 1
                    it = 0
                    for ti in range(8):
                        for cb in range(4):
                            pb_ = 2 + it % 6
                            cs_ = slice(cb * 512, (cb + 1) * 512)
                            P.pe([(psb[pb_], uT[:, j, ti * 128:(ti + 1) * 128], w2[:, j, cs_], j == 0, j == 3, None) for j in range(4)],
                                 reads=["w2"] + [("uT", j, ti // 4) for j in range(4)], writes=[PSR(pb_)])
                            if hb == 0:
                                P.copy("dve", acc[:, ti, cs_], psb[pb_], reads=[PSR(pb_)], writes=[("acc", ti, cb)])
                            else:
                                P.tt("dve", acc[:, ti, cs_], acc[:, ti, cs_], psb[pb_], ALU.add, reads=[PSR(pb_), ("acc", ti, cb)], writes=[("acc", ti, cb)])
                            it += 1
                xb3_ = P.alloc(F32, [D])
                xb3 = [xb3_, xb3_]
                for ti in range(8):
                    xt = xb3[ti % 2]
                    P.dma("sp", xt, yg[ti * 128:(ti + 1) * 128, :], reads=[("yscr", ti)], writes=[("xb3", 0)])
                    P.tt("dve", acc[:, ti, :], acc[:, ti, :], g2bc, ALU.mult, reads=[("acc", ti, cb) for cb in range(4)] + ["gatebc"],
                         writes=[("acc", ti, cb) for cb in range(4)])
                    P.tt("pool", xt, xt, acc[:, ti, :], ALU.add, reads=[("xb3", 0)] + [("acc", ti, cb) for cb in range(4)], writes=[("xb3", 0)])
                    P.dma("sp", yg[ti * 128:(ti + 1) * 128, :], xt, reads=[("xb3", 0), ("yscr", ti)], writes=[("yscr", ti)])
                P.release(mF)

            P.release(m_h)
            P.limit = ARENA_WORDS
            phase_F()

        def rope64(src, dst, tmp, tile, reads, writes):
            Cc = ropeC[:, tile, :]
            Sg = ropeS[:, tile, :]
            s4 = src.rearrange("p (a h d) -> p a h d", a=2, h=2)
            t4 = tmp.rearrange("p (a h d) -> p a h d", a=2, h=2)
            g4 = Sg.rearrange("p (a h d) -> p a h d", a=2, h=2)
            P.tt("dve", t4[:, :, 0, :], s4[:, :, 1, :], g4[:, :, 0, :], ALU.mult, reads=list(reads) + ["ct"], writes=["ropetmp"])
            P.tt("dve", t4[:, :, 1, :], s4[:, :, 0, :], g4[:, :, 1, :], ALU.mult, reads=list(reads) + ["ct", "ropetmp"], writes=["ropetmp"])
            P.tt("dve", dst, src, Cc, ALU.mult, reads=list(reads) + ["ct"], writes=list(writes))
            P.tt("dve", dst, dst, tmp, ALU.add, reads=["ropetmp"] + list(writes), writes=list(writes))

        for gi in range(2):
            run_group(gi)
            S.barrier()

        S.finish("sp")
        with nc.Block() as block:
            S.emit(sems, block)
    return P


_CACHE = {}


def make_in_maps(inputs):
    f = lambda a: np.ascontiguousarray(np.asarray(a, dtype=np.float32))
    ct = make_consts()
    shared = {
        "ctab": ct,
        "norm1": f(inputs["norm1"][0]), "w_ada": f(inputs["w_ada"][0]), "b_ada": f(inputs["b_ada"][0]),
        "w_in": f(inputs["w_in"][0]), "q_norm": f(inputs["q_norm"][0]), "kv_norm": f(inputs["kv_norm"][0]),
        "w_kv_up": f(inputs["w_kv_up"][0]), "k_norm": f(inputs["k_norm"][0]), "conv_rkv": f(inputs["conv_rkv"][0]),
        "k_k": f(inputs["k_k"][0]), "k_a": f(inputs["k_a"][0]), "r_k": f(inputs["r_k"][0].reshape(-1)),
        "w0": f(inputs["w0"][0]), "w_up": f(inputs["w_up"][0]), "a0": f(inputs["a0"][0]), "a_up": f(inputs["a_up"][0]),
        "g_up": f(inputs["g_up"][0]), "lnx_w": f(inputs["lnx_w"][0]), "lnx_b": f(inputs["lnx_b"][0]),
        "w_br_mla": f(inputs["w_br_mla"][0]), "w_br_rwkv": f(inputs["w_br_rwkv"][0]), "w_out": f(inputs["w_out"][0]),
        "norm2": f(inputs["norm2"][0]), "w_ff_in": f(inputs["w_ff_in"][0]), "w_ff_out": f(inputs["w_ff_out"][0]),
    }
    maps = []
    for i in range(8):
        m = dict(shared)
        m["xp"] = f(inputs["x_prompt"][4 * i:4 * i + 4].reshape(NT, D))
        m["xs"] = f(inputs["x_sample"][i])
        m["cckv"] = f(inputs["cache_mla_ckv"][i, 0])
        m["ckr"] = f(inputs["cache_mla_kr"][i, 0])
        m["st"] = f(inputs["state_rwkv"][i, 0])
        m["cond"] = f(np.stack([inputs["c_ctx"], inputs["c"][i]], axis=0))
        maps.append(m)
    return maps


def kernel(**inputs):
    if "prog" not in _CACHE:
        _CACHE["prog"] = build_program()
    P = _CACHE["prog"]
    in_maps = make_in_maps(inputs)
    res = run_bass_kernel_spmd(P.nc, in_maps, core_ids=list(range(8)))
    r = res.results
    yp = np.concatenate([r[i]["yp"].reshape(4, 256, D) for i in range(8)], axis=0)
    ys = np.stack([r[i]["ys"] for i in range(8)], axis=0)
    nckv = np.concatenate([r[i]["nckv"].reshape(4, 1, 256, 512) for i in range(8)], axis=0)
    nkr = np.concatenate([r[i]["nkr"].reshape(4, 1, 256, 64) for i in range(8)], axis=0)
    nst = np.concatenate([r[i]["nst"].reshape(4, 1, 2, 32, 64, 64) for i in range(8)], axis=0)
    return (yp.astype(np.float32), ys.astype(np.float32), nckv.astype(np.float32), nkr.astype(np.float32),
            nst.astype(np.float32))
```

```python
from contextlib import ExitStack
import numpy as np
import concourse.bass as bass
import concourse.mybir as mybir
from concourse.bass_utils import run_bass_kernel_spmd

F32 = mybir.dt.float32
BF16 = mybir.dt.bfloat16
AF = mybir.ActivationFunctionType
ALU = mybir.AluOpType

ENGS = ("pe", "act", "dve", "pool", "sp")
N_DMA_SEMS = 28
SEM_CH = 16000
N_ENG_SEMS = 8


class Sched:
    def __init__(self):
        self.streams = {e: [] for e in ENGS}
        self.count = {e: 0 for e in ENGS}
        self.known = {e: {} for e in ENGS}
        self.last_w = {}
        self.readers = {}
        self.dma_i = 0
        self.dma_qi = {"sp": 0, "pool": 0}
        self.dma_cnt = [0] * (2 * N_DMA_SEMS)
        self.dma_bank = 0

    def _deps(self, reads, writes):
        toks = []
        for r in reads:
            t = self.last_w.get(r)
            if t is not None:
                toks.append(t)
        for w in writes:
            t = self.last_w.get(w)
            if t is not None:
                toks.append(t)
            toks.extend(self.readers.get(w, ()))
        return toks

    def _commit(self, tok, reads, writes):
        for r in reads:
            self.readers.setdefault(r, []).append(tok)
        for w in writes:
            self.last_w[w] = tok
            self.readers[w] = []

    def _waits(self, eng, toks):
        need = {}
        kn = self.known[eng]
        for (k, v) in toks:
            if kn.get(k, 0) >= v:
                continue
            if need.get(k, 0) < v:
                need[k] = v
        for k, v in need.items():
            kn[k] = v
        return list(need.items())

    def op(self, eng, fn, reads=(), writes=()):
        toks = self._deps(reads, writes)
        waits = self._waits(eng, toks)
        n = self.count[eng]
        self.count[eng] += 1
        key = (eng, n // SEM_CH)
        tok = (key, n % SEM_CH + 1)
        self.streams[eng].append((waits, fn, (key, 1)))
        self._commit(tok, reads, writes)
        return tok

    def dma(self, eng, fn, reads=(), writes=()):
        toks = self._deps(reads, writes)
        qb = 0 if eng == "sp" else 1
        s = qb * N_DMA_SEMS + self.dma_qi[eng] % N_DMA_SEMS
        self.dma_qi[eng] += 1
        self.dma_i += 1
        key = ("dma", s)
        if self.dma_cnt[s] > 0:
            toks.append((key, 16 * self.dma_cnt[s]))
        waits = self._waits(eng, toks)
        self.dma_cnt[s] += 1
        tok = (key, 16 * self.dma_cnt[s])
        self.streams[eng].append((waits, fn, (key, 16)))
        self._commit(tok, reads, writes)
        return tok

    def _all_toks(self):
        toks = [((e, (self.count[e] - 1) // SEM_CH), (self.count[e] - 1) % SEM_CH + 1)
                for e in ENGS if self.count[e] > 0]
        toks += [(("dma", s), 16 * c) for s, c in enumerate(self.dma_cnt) if c > 0]
        return toks

    def barrier(self):
        toks = self._all_toks()
        for e in ENGS:
            waits = self._waits(e, toks)
            if waits:
                self.streams[e].append((waits, None, None))

    def finish(self, eng="sp"):
        waits = self._waits(eng, self._all_toks())
        if waits:
            self.streams[eng].append((waits, None, None))

    def clear_streams(self):
        self.streams = {e: [] for e in ENGS}

    def emit(self, sems, block):
        def run(engobj, ename):
            for waits, fn, inc in self.streams[ename]:
                for k, v in waits:
                    engobj.wait_ge(sems[k], v)
                if fn is not None:
                    ins = fn(engobj)
                    ins.then_inc(sems[inc[0]], inc[1])

        @block.tensor
        def _(e):
            run(e, "pe")

        @block.scalar
        def _(e):
            run(e, "act")

        @block.vector
        def _(e):
            run(e, "dve")

        @block.gpsimd
        def _(e):
            run(e, "pool")

        @block.sync
        def _(e):
            run(e, "sp")


D = 2048
NT = 1024
KC = 16
H_MLA = 16
KV_RANK = 512
IN_DIM = 14272
OFF_Q, OFF_CKV, OFF_KR, OFF_RKV, OFF_WD, OFF_AD, OFF_GD, OFF_GATE = 0, 3072, 3584, 3648, 9792, 9920, 10048, 10176
EPS = 1e-6
LNX_EPS = 64e-5
C = 64
NCH = NT // C
SQ2048 = float(np.sqrt(2048.0))

CT_IDENT, CT_MASK0, CT_MASK1, CT_BONES, CT_BAVG, CT_SCAN, CT_ROPEC, CT_ROPES, CT_EPS, CT_PERM, CT_END = (
    0, 128, 448, 768, 896, 1024, 2048, 3072, 4096, 4104, 4168)


def make_consts():
    ct = np.zeros((128, CT_END), np.float32)
    ct[:, CT_IDENT:CT_IDENT + 128] = np.eye(128, dtype=np.float32)
    idx = np.arange(64)
    for d, base in ((0, CT_MASK0), (1, CT_MASK1)):
        if d == 0:
            t_gt_i_rows_t = (idx[:, None] > idx[None, :])
            t_gt_i_rows_i = (idx[None, :] > idx[:, None])
            t_ge_i_rows_i = (idx[None, :] >= idx[:, None])
        else:
            t_gt_i_rows_t = (idx[:, None] < idx[None, :])
            t_gt_i_rows_i = (idx[None, :] < idx[:, None])
            t_ge_i_rows_i = (idx[None, :] <= idx[:, None])
        blocks = [-1.0 * t_gt_i_rows_t, -1.0 * t_gt_i_rows_i, 1.0 * t_gt_i_rows_i,
                  1.0 * t_ge_i_rows_i, -1.0 * t_ge_i_rows_i]
        m = np.concatenate([b.astype(np.float32) for b in blocks], axis=1)
        ct[0:64, base:base + 320] = m
        ct[64:128, base:base + 320] = m
    bo = np.zeros((128, 128), np.float32)
    bo[0:64, 0:64] = 1.0
    bo[64:128, 64:128] = 1.0
    ct[:, CT_BONES:CT_BONES + 128] = bo
    ct[:, CT_BAVG:CT_BAVG + 128] = bo / 64.0
    sm = np.ones(1024, np.float32)
    sm[::64] = 0.0
    ct[:, CT_SCAN:CT_SCAN + 1024] = sm[None, :]
    t = np.arange(1024)
    row = (t // 64).astype(np.float32)
    col = (t % 64).astype(np.float32)
    inv = np.power(np.float32(10000.0), -np.arange(16, dtype=np.float32) / np.float32(16)).astype(np.float32)
    ar = (row[:, None] * inv[None, :]).astype(np.float32)
    ac = (col[:, None] * inv[None, :]).astype(np.float32)
    cr, sr, cc, sc = np.cos(ar), np.sin(ar), np.cos(ac), np.sin(ac)
    Ct = np.concatenate([cr, cr, cc, cc], axis=1).astype(np.float32)
    St = np.concatenate([-sr, sr, -sc, sc], axis=1).astype(np.float32)
    ct[0:64, CT_ROPEC:CT_ROPEC + 1024] = Ct.T
    ct[0:64, CT_ROPES:CT_ROPES + 1024] = St.T
    pm = np.zeros((64, 64), np.float32)
    for dd in range(64):
        g0 = (dd // 32) * 32
        o = dd - g0
        pm[g0 + (o + 16) % 32, dd] = 1.0
    ct[0:64, CT_PERM:CT_PERM + 64] = pm
    ct[:, CT_EPS + 0] = 2048.0 * EPS
    ct[:, CT_EPS + 1] = EPS
    ct[:, CT_EPS + 2] = 1e-12
    ct[:, CT_EPS + 3] = LNX_EPS
    ct[:, CT_EPS + 4] = 192.0 * EPS
    return ct


ARENA_WORDS = 53120


class Prog:
    def __init__(self, debug=()):
        self.debug = set(debug)
        self.nc = bass.Bass("TRN2", target_bir_lowering=False)
        self.S = Sched()
        self.top = 0
        self.limit = ARENA_WORDS
        self.dbg_outs = {}

    def alloc(self, dtype, shape):
        n = int(np.prod(shape))
        words = n if dtype == F32 else (n + 1) // 2
        words = (words + 7) // 8 * 8
        off = self.top
        self.top += words
        assert self.top <= self.limit, f"arena overflow {self.top} > {self.limit}"
        ap = self.arena[:, off:off + (n if dtype == F32 else (n + 1) // 2)]
        if dtype != F32:
            ap = ap.bitcast(BF16)
            if n % 2:
                ap = ap[:, 0:n]
        if len(shape) == 2:
            ap = ap.rearrange("p (a b) -> p a b", a=shape[0])
        elif len(shape) == 3:
            ap = ap.rearrange("p (a b c) -> p a b c", a=shape[0], b=shape[1])
        elif len(shape) == 4:
            ap = ap.rearrange("p (a b c d) -> p a b c d", a=shape[0], b=shape[1], c=shape[2])
        return ap

    def mark(self):
        return self.top

    def release(self, m, barrier=True):
        if barrier and self.top > m:
            self.S.barrier()
        self.top = m

    def dma(self, eng, out, in_, reads=(), writes=()):
        self.S.dma(eng, lambda e: e.dma_start(out=out, in_=in_), reads, writes)

    def act(self, out, in_, func, reads=(), writes=(), **kw):
        self.S.op("act", lambda e: e.activation(out=out, in_=in_, func=func, **kw), reads, writes)

    def tt(self, eng, out, in0, in1, op, reads=(), writes=()):
        self.S.op(eng, lambda e: e.tensor_tensor(out=out, in0=in0, in1=in1, op=op), reads, writes)

    def ts(self, eng, out, in0, s1, op0, s2=None, op1=None, reads=(), writes=(), accum_out=None):
        if op1 is None:
            if accum_out is None:
                self.S.op(eng, lambda e: e.tensor_scalar(out=out, in0=in0, scalar1=s1, scalar2=None, op0=op0), reads, writes)
            else:
                self.S.op(eng, lambda e: e.tensor_scalar(out=out, in0=in0, scalar1=s1, scalar2=None, op0=op0, accum_out=accum_out), reads, writes)
        else:
            self.S.op(eng, lambda e: e.tensor_scalar(out=out, in0=in0, scalar1=s1, scalar2=s2, op0=op0, op1=op1), reads, writes)

    def stt(self, eng, out, in0, scalar, in1, op0, op1, reads=(), writes=()):
        self.S.op(eng, lambda e: e.scalar_tensor_tensor(out=out, in0=in0, scalar=scalar, in1=in1, op0=op0, op1=op1), reads, writes)

    def copy(self, eng, out, in_, reads=(), writes=()):
        if eng == "act":
            self.S.op("act", lambda e: e.activation(out=out, in_=in_, func=AF.Copy), reads, writes)
        else:
            self.S.op(eng, lambda e: e.tensor_copy(out=out, in_=in_), reads, writes)

    def memset(self, eng, ap, val, writes=()):
        self.S.op(eng, lambda e: e.memset(ap, val), (), writes)

    def recip(self, out, in_, reads=(), writes=()):
        self.S.op("dve", lambda e: e.reciprocal(out=out, in_=in_), reads, writes)

    def pe(self, mms, reads=(), writes=()):
        mms = list(mms)

        def fn(e):
            ins = None
            for (o, l, r, st, sp, tp) in mms:
                if tp is None:
                    ins = e.matmul(o, l, r, start=st, stop=sp)
                else:
                    ins = e.matmul(o, l, r, start=st, stop=sp, tile_position=tp)
            return ins
        self.S.op("pe", fn, reads, writes)

    def dbg(self, name, ap_sb, shape, reads):
        if name not in self.debug:
            return
        d = self.nc.dram_tensor("dbg_" + name, [128] + list(shape), ap_sb.dtype, kind="ExternalOutput").ap()
        self.dbg_outs[name] = d
        self.dma("sp", d, ap_sb, reads=reads)


def build_program(debug=(), stop_after=None):
    P = Prog(debug)
    nc = P.nc
    S = P.S

    def din(name, shape):
        return nc.dram_tensor(name, shape, F32, kind="ExternalInput").ap()

    def dout(name, shape):
        return nc.dram_tensor(name, shape, F32, kind="ExternalOutput").ap()

    xin = [din("xp", [NT, D]), din("xs", [NT, D])]
    cckv = din("cckv", [256, 512])
    ckr = din("ckr", [256, 64])
    st_in = din("st", [2, 32, 64, 64])
    cond = din("cond", [2, D])
    ctab = din("ctab", [128, CT_END])
    norm1 = din("norm1", [D]); w_ada = din("w_ada", [D, 6 * D]); b_ada = din("b_ada", [6 * D])
    w_in = din("w_in", [D, IN_DIM]); q_norm = din("q_norm", [192]); kv_norm = din("kv_norm", [512])
    w_kv_up = din("w_kv_up", [512, 4096]); k_norm = din("k_norm", [192]); conv_rkv = din("conv_rkv", [3, 6144])
    k_k = din("k_k", [D]); k_a = din("k_a", [D]); r_k = din("r_k", [D])
    w0 = din("w0", [2, D]); w_up = din("w_up", [2, 64, D]); a0 = din("a0", [2, D]); a_up = din("a_up", [2, 64, D])
    g_up = din("g_up", [128, D]); lnx_w = din("lnx_w", [D]); lnx_b = din("lnx_b", [D])
    w_br_mla = din("w_br_mla", [D, D]); w_br_rwkv = din("w_br_rwkv", [D, D]); w_out = din("w_out", [D, D])
    norm2 = din("norm2", [D]); w_ff_in = din("w_ff_in", [D, 4 * D]); w_ff_out = din("w_ff_out", [4 * D, D])

    yout = [dout("yp", [NT, D]), dout("ys", [NT, D])]
    nckv = dout("nckv", [NT, 512])
    nkr = dout("nkr", [NT, 64])
    nst = dout("nst", [4, 2, 32, 64, 64])

    es = ExitStack()
    with es:
        keys = [(e, i) for e in ENGS for i in range(N_ENG_SEMS if e != "sp" else 1)] + [("dma", s) for s in range(2 * N_DMA_SEMS)]
        sems = {k: es.enter_context(nc.semaphore(f"s_{k[0]}_{k[1]}")) for k in keys}
        P.arena = es.enter_context(nc.sbuf_tensor("arena", [128, ARENA_WORDS], F32))
        psall = es.enter_context(nc.psum_tensor("psall", [128, 4096], F32))
        psb = [psall[:, 512 * i:512 * (i + 1)] for i in range(8)]

        def PSR(i):
            return ("ps", i)

        ct = P.alloc(F32, [CT_END])
        P.dma("sp", ct, ctab[:, :], writes=["ct"])
        identf = ct[:, CT_IDENT:CT_IDENT + 128]
        bones = ct[:, CT_BONES:CT_BONES + 128]
        bavg = ct[:, CT_BAVG:CT_BAVG + 128]
        scanmask = ct[:, CT_SCAN:CT_SCAN + 1024]
        ropeCT = ct[0:64, CT_ROPEC:CT_ROPEC + 1024]
        ropeST = ct[0:64, CT_ROPES:CT_ROPES + 1024]
        permf = ct[0:64, CT_PERM:CT_PERM + 64]
        maskd = [ct[:, CT_MASK0:CT_MASK0 + 320], ct[:, CT_MASK1:CT_MASK1 + 320]]
        eps2048 = ct[:, CT_EPS + 0:CT_EPS + 1]
        eps1 = ct[:, CT_EPS + 1:CT_EPS + 2]
        eps12 = ct[:, CT_EPS + 2:CT_EPS + 3]
        epslnx = ct[:, CT_EPS + 3:CT_EPS + 4]
        eps192 = ct[:, CT_EPS + 4:CT_EPS + 5]
        ident = P.alloc(BF16, [128])
        P.copy("dve", ident, identf, reads=["ct"], writes=["ident"])
        ones_bf = P.alloc(BF16, [128])
        P.memset("pool", ones_bf, 1.0, writes=["ones_bf"])
        I2 = P.alloc(F32, [64])
        P.tt("dve", I2, identf[:, 0:64], identf[:, 64:128], ALU.add, reads=["ct"], writes=["I2"])

        cv = {}
        stage = P.alloc(F32, [128])
        cvn = 0

        def colvec(name, dram_rows_ap, n):
            nonlocal cvn
            dst = P.alloc(F32, [n])
            P.dma("sp", stage[0:n, :], dram_rows_ap, reads=(), writes=["stage"])
            P.pe([(psb[7][:, 0:n], stage[0:n, :], identf[0:n, 0:n], True, True, None)],
                 reads=["stage", "ct"], writes=[PSR(7)])
            P.copy("dve", dst, psb[7][:, 0:n], reads=[PSR(7)], writes=["cv_" + name])
            cv[name] = dst
            cvn += 1

        colvec("b_ada", b_ada.rearrange("(n p) -> n p", p=128), 96)
        colvec("norm1", norm1.rearrange("(n p) -> n p", p=128), 16)
        colvec("norm2", norm2.rearrange("(n p) -> n p", p=128), 16)
        colvec("kv_norm", kv_norm.rearrange("(n p) -> n p", p=128), 4)
        for tap in range(3):
            colvec(f"conv{tap}", conv_rkv[tap].rearrange("(n p) -> n p", p=128), 48)
        colvec("k_k", k_k.rearrange("(n p) -> n p", p=128), 16)
        colvec("k_a", k_a.rearrange("(n p) -> n p", p=128), 16)
        colvec("r_k", r_k.rearrange("(n p) -> n p", p=128), 16)
        colvec("w0", w0.rearrange("d (n p) -> (d n) p", p=128), 32)
        colvec("a0", a0.rearrange("d (n p) -> (d n) p", p=128), 32)
        colvec("lnx_w", lnx_w.rearrange("(n p) -> n p", p=128), 16)
        colvec("lnx_b", lnx_b.rearrange("(n p) -> n p", p=128), 16)
        CVR = ["cv_" + k for k in cv]
        omka = P.alloc(F32, [16])
        P.ts("dve", omka, cv["k_a"], -1.0, ALU.mult, s2=1.0, op1=ALU.add, reads=["cv_k_a"], writes=["omka"])

        gcol = P.alloc(F32, [4])
        P.memset("dve", gcol, 0.0, writes=["gcol"])
        for ci_, (src_, n_) in enumerate(((q_norm[0:128], 128), (q_norm[128:192], 64), (k_norm[0:128], 128), (k_norm[128:192], 64))):
            P.dma("sp", stage[0:1, 0:n_], src_.rearrange("(o n) -> o n", o=1), writes=["stage"])
            P.pe([(psb[7][0:n_, 0:1], stage[0:1, 0:n_], identf[0:1, 0:1], True, True, None)], reads=["stage", "ct"], writes=[PSR(7)])
            P.copy("dve", gcol[0:n_, ci_:ci_ + 1], psb[7][0:n_, 0:1], reads=[PSR(7)], writes=["gcol"])
        P.ts("dve", gcol[:, 0:2], gcol[:, 0:2], float(192.0 ** -0.5), ALU.mult, reads=["gcol"], writes=["gcol"])

        modT = P.alloc(F32, [96, 2])
        scT = P.alloc(BF16, [KC, 2])
        g1 = P.alloc(F32, [2, KC]); g2 = P.alloc(F32, [2, KC])
        sh1 = P.alloc(F32, [2, KC]); sh2 = P.alloc(F32, [2, KC])
        m0 = P.mark()
        craw = P.alloc(F32, [D])
        P.dma("sp", craw[0:2, :], cond[:, :], writes=["craw"])
        P.act(craw[0:2, :], craw[0:2, :], AF.Silu, reads=["craw"], writes=["craw"])
        P.pe([(psb[7][:, 2 * kc:2 * kc + 2], craw[0:2, kc * 128:(kc + 1) * 128], identf[0:2, 0:2], True, True, None)
              for kc in range(KC)], reads=["craw", "ct"], writes=[PSR(7)])
        P.copy("dve", scT, psb[7][:, 0:2 * KC].rearrange("p (a b) -> p a b", a=KC), reads=[PSR(7)], writes=["scT"])
        wb = [P.alloc(BF16, [KC, 512]) for _ in range(4)]
        nblk = 0
        for sixth in range(6):
            for cb in range(4):
                col0 = sixth * D + cb * 512
                buf = wb[nblk % 4]
                br = ("wb", nblk % 4)
                P.dma("pool", buf, w_ada[:, col0:col0 + 512].rearrange("(kc p) n -> p kc n", p=128), writes=[br])
                mms = []
                for fb in range(4):
                    blk = (col0 // 128) + fb
                    for kc in range(KC):
                        mms.append((psb[6][:, 2 * blk:2 * blk + 2], buf[:, kc, fb * 128:(fb + 1) * 128], scT[:, kc, :],
                                    kc == 0, kc == KC - 1, None))
                P.pe(mms, reads=[br, "scT"], writes=[("modps", nblk)])
                nblk += 1
        P.memset("dve", modT, 0.0, writes=["modT"])
        for sixth in range(6):
            P.tt("dve", modT[:, sixth * 16:(sixth + 1) * 16, :], psb[6][:, sixth * 32:(sixth + 1) * 32].rearrange("p (a b) -> p a b", b=2),
                 cv["b_ada"][:, sixth * 16:(sixth + 1) * 16].unsqueeze(2).to_broadcast([128, 16, 2]), ALU.add,
                 reads=[("modps", i) for i in range(nblk)] + ["cv_b_ada"], writes=["modT"])
        for j in range(2):
            for (gdst, shdst, sixth_sh, sixth_sc, nrm) in ((g1, sh1, 0, 1, "norm1"), (g2, sh2, 3, 4, "norm2")):
                P.ts("dve", gdst[:, j, :], modT[:, sixth_sc * 16:(sixth_sc + 1) * 16, j], 1.0, ALU.add,
                     reads=["modT"], writes=["gsh"])
                P.tt("dve", gdst[:, j, :], gdst[:, j, :], cv[nrm], ALU.mult, reads=["gsh", "cv_" + nrm], writes=["gsh"])
                P.ts("dve", gdst[:, j, :], gdst[:, j, :], SQ2048, ALU.mult, reads=["gsh"], writes=["gsh"])
                P.copy("dve", shdst[:, j, :], modT[:, sixth_sh * 16:(sixth_sh + 1) * 16, j], reads=["modT"], writes=["gsh"])
        P.release(m0)
        S.barrier()
        S.dma_bank = 1
        P.dbg("modT", modT, [96, 2], ["modT"])
        P.dbg("g1", g1, [2, KC], ["gsh"])

        persist_top = P.mark()

        def run_group(gi):
            xg = xin[gi]
            yg = yout[gi]
            cj = gi
            nseq, T = ((4, 256) if gi == 0 else (1, 1024))
            has_cache = (gi == 1)
            P.release(persist_top)
            hT = P.alloc(BF16, [KC, NT])
            m_h = P.mark()

            def norm_to_T(src_tiles_fn, gcol, shcol, dstT, dst_region):
                mA = P.mark()
                xn = [P.alloc(BF16, [D]) for _ in range(2)]
                ssq = P.alloc(F32, [16])
                for ti in range(8):
                    xt, xr = src_tiles_fn(ti)
                    P.act(xn[ti % 2], xt, AF.Square, reads=[xr], writes=[("ssq", ti), ("xn", ti % 2)], accum_out=ssq[:, ti:ti + 1])
                    P.act(ssq[:, 8 + ti:9 + ti], ssq[:, ti:ti + 1], AF.Sqrt, reads=[("ssq", ti), "ct"], writes=[("ssq2", ti)],
                          bias=eps2048, scale=1.0)
                    P.recip(ssq[:, 8 + ti:9 + ti], ssq[:, 8 + ti:9 + ti], reads=[("ssq2", ti)], writes=[("ssq2", ti)])
                    xb = xn[ti % 2]
                    P.ts("dve", xb, xt, ssq[:, 8 + ti:9 + ti], ALU.mult, reads=[xr, ("ssq2", ti)], writes=[("xn", ti % 2)])
                    for q in range(4):
                        pst = psb[q % 2]
                        P.pe([(pst[:, j * 128:(j + 1) * 128], xb[:, (4 * q + j) * 128:(4 * q + j + 1) * 128], ident, True, True, None)
                              for j in range(4)], reads=[("xn", ti % 2), "ident"], writes=[PSR(q % 2)])
                        for j in range(4):
                            kc = 4 * q + j
                            if j % 2 == 0:
                                P.act(dstT[:, kc, ti * 128:(ti + 1) * 128], pst[:, j * 128:(j + 1) * 128], AF.Identity,
                                      reads=[PSR(q % 2), "gsh"], writes=[(dst_region, ti)],
                                      scale=gcol[:, kc:kc + 1], bias=shcol[:, kc:kc + 1])
                            else:
                                P.ts("dve", dstT[:, kc, ti * 128:(ti + 1) * 128], pst[:, j * 128:(j + 1) * 128], gcol[:, kc:kc + 1], ALU.mult,
                                     s2=shcol[:, kc:kc + 1], op1=ALU.add, reads=[PSR(q % 2), "gsh"], writes=[(dst_region, ti)])
                if f"ps{gi}" in P.debug:
                    pdump = P.alloc(F32, [512])
                    P.copy("dve", pdump, psb[1][:, :], reads=[PSR(1)], writes=["pdump"])
                    P.dbg(f"ps{gi}", pdump, [512], ["pdump"])
                    P.dbg(f"ident{gi}", ident, [128], ["ident"])
                P.dbg(f"ssq{gi}", ssq, [16], [("ssq2", t_) for t_ in range(8)])
                P.dbg(f"xn{gi}", xn[1], [D], [("xn", 1)])
                P.release(mA)

            mX = P.mark()
            xbuf = [P.alloc(F32, [D]) for _ in range(2)]

            def load_x(ti):
                P.dma("sp", xbuf[ti % 2], xg[ti * 128:(ti + 1) * 128, :], writes=[("xbuf", ti % 2)])
                return xbuf[ti % 2], ("xbuf", ti % 2)
            norm_to_T(load_x, g1[:, cj, :], sh1[:, cj, :], hT, "hT")
            P.release(mX)
            HT_R = [("hT", ti) for ti in range(8)]
            P.dbg(f"hT{gi}", hT, [KC, NT], HT_R)
            if stop_after == "A":
                return

            orwkvT = P.alloc(BF16, [16, NT])
            def phase_R():
                CPS = T // C
                mR = P.mark()
                twd_m = P.alloc(BF16, [NT])
                ad_m = P.alloc(BF16, [NT])
                sgdT = P.alloc(BF16, [NT])
                mW = P.mark()
                wl = P.alloc(BF16, [KC, 384])
                P.dma("pool", wl, w_in[:, OFF_WD:OFF_WD + 384].rearrange("(kc p) n -> p kc n", p=128), writes=["wl"])
                for cb in range(3):
                    for half in range(2):
                        ps = psb[half]
                        P.pe([(ps, wl[:, kc, cb * 128:(cb + 1) * 128], hT[:, kc, half * 512:(half + 1) * 512], kc == 0, kc == KC - 1, None)
                              for kc in range(KC)], reads=["wl"] + HT_R, writes=[PSR(half)])
                        hs = slice(half * 512, (half + 1) * 512)
                        if cb == 0:
                            P.act(twd_m[:, hs], ps, AF.Tanh, reads=[PSR(half)], writes=["twd_m"])
                        elif cb == 1:
                            P.act(ad_m[:, hs], ps, AF.Copy, reads=[PSR(half)], writes=["ad_m"])
                        else:
                            P.act(sgdT[:, hs], ps, AF.Sigmoid, reads=[PSR(half)], writes=["sgdT"])
                P.release(mW)
                wrk = [P.alloc(BF16, [KC, 128]) for _ in range(3)]
                wsm2 = [P.alloc(BF16, [5, 128]) for _ in range(2)]
                zpad = P.alloc(F32, [3, nseq * (T + 2)])
                T456 = [P.alloc(F32, [NT]) for _ in range(3)]
                rkv = P.alloc(F32, [3, NT])
                kk = P.alloc(BF16, [NT]); ysum = P.alloc(F32, [NT])
                vtm_b = [P.alloc(BF16, [NCH, 64]) for _ in range(2)]
                fmKR = [P.alloc(BF16, [2, NT]) for _ in range(2)]
                fmKB0 = P.alloc(BF16, [2, NT])
                gn_tmp = P.alloc(F32, [NT])
                fmKB1 = gn_tmp.bitcast(BF16).rearrange("p (a b) -> p a b", a=2)
                fmKBd = [fmKB0, fmKB1]
                fmKB = fmKBd[0]

                class FMView:
                    def __init__(self, kr, kb):
                        self.kr = kr
                        self.kb = kb

                    def __getitem__(self, key):
                        p_, slot_, c_ = key
                        return (self.kr if slot_ < 2 else self.kb)[p_, slot_ % 2, c_]
                FMV = [FMView(fmKR[0], fmKBd[0]), FMView(fmKR[1], fmKBd[1])]
                ktm_d = [P.alloc(BF16, [NCH, 64]) for _ in range(2)]; nbtm_d = [P.alloc(BF16, [NCH, 64]) for _ in range(2)]
                st0_d = [P.alloc(BF16, [NCH, 3, 64]) for _ in range(2)]
                YX1 = P.alloc(BF16, [NCH, 2, 64])
                YX = [YX1, YX1]
                NTf = P.alloc(F32, [NCH, 64]); NTb_d = [P.alloc(BF16, [NCH, 64]) for _ in range(2)]
                gam_d = [P.alloc(F32, [NCH]) for _ in range(2)]
                NCHAIN = 2 * nseq
                A0 = P.alloc(F32, [NCHAIN, 64]); A0g = P.alloc(F32, [NCHAIN, 64]); A0b = P.alloc(BF16, [NCHAIN, 64])
                rhsb = P.alloc(BF16, [8, 64]); pb = P.alloc(BF16, [8, 64])
                Tn = [zpad[:, j, 0:NT] for j in range(3)] + T456
                TR = ["T1", "T2", "T3", "T4", "T5", "T6"]
                zp4 = zpad.rearrange("p j (s t) -> p j s t", s=nseq)
                for wb_ in range(2):
                    P.memset("pool", wsm2[wb_], 0.0, writes=[("wsm", wb_)])

                def load_hp_weights(hp_):
                    c0_ = hp_ * 128
                    wsm_ = wsm2[hp_ % 2]
                    wr_ = ("wsm", hp_ % 2)
                    for j in range(3):
                        col = OFF_RKV + j * 2048 + c0_
                        P.dma("pool", wrk[j], w_in[:, col:col + 128].rearrange("(kc p) n -> p kc n", p=128), writes=[("wrk", j)])
                    for d in range(2):
                        P.dma("pool", wsm_[64 * d:64 * d + 64, d, :], w_up[d, :, c0_:c0_ + 128], writes=[wr_])
                        P.dma("pool", wsm_[64 * d:64 * d + 64, 2 + d, :], a_up[d, :, c0_:c0_ + 128], writes=[wr_])
                    P.dma("pool", wsm_[:, 4, :], g_up[:, c0_:c0_ + 128], writes=[wr_])
                load_hp_weights(0)
                FM_R = ["fm0", "fm1", "fm2", "fm3"]

                def hd(out_fn, lhs_fn, rhs_fn, st, sp):
                    return [(out_fn(e), lhs_fn(e), rhs_fn(e), st, sp, (64 * e, 64 * e)) for e in range(2)]

                def sl(e):
                    return slice(64 * e, 64 * e + 64)

                def hp_ctx(hp):
                    c0 = hp * 128
                    wsm = wsm2[hp % 2]
                    wsmR = ("wsm", hp % 2)
                    r_, k_, v_ = rkv[:, 0, :], rkv[:, 1, :], rkv[:, 2, :]
                    vtm = vtm_b[hp % 2]
                    vtmR = ("vtm", hp % 2)

                    def gen_z():
                        for j in range(3):
                            P.memset("pool", zp4[:, j, :, 0:1], 0.0, writes=[TR[j]])
                            yield
                            P.memset("pool", zp4[:, j, :, T + 1:T + 2], 0.0, writes=[TR[j]])
                            yield
                            wbuf = wrk[j]
                            wreg = ("wrk", j)
                            for half in range(2):
                                ps = psb[half]
                                P.pe([(ps, wbuf[:, kc, :], hT[:, kc, half * 512:(half + 1) * 512], kc == 0, kc == KC - 1, None)
                                      for kc in range(KC)], reads=[wreg] + HT_R, writes=[PSR(half)])
                                yield
                                spc = 512 // T if T < 512 else 1
                                if T >= 512:
                                    P.copy("act", zp4[:, j, 0, 1 + half * 512:1 + (half + 1) * 512], ps, reads=[PSR(half)], writes=[TR[j]])
                                    yield
                                else:
                                    P.copy("act", zp4[:, j, half * spc:(half + 1) * spc, 1:T + 1],
                                           ps.rearrange("p (s t) -> p s t", s=spc), reads=[PSR(half)], writes=[TR[j]])
                                    yield
                            o3 = rkv[:, j, :].rearrange("p (s t) -> p s t", s=nseq)
                            cc = j * 16 + hp
                            P.ts("dve", o3, zp4[:, j, :, 1:T + 1], cv["conv1"][:, cc:cc + 1], ALU.mult, reads=[TR[j], "cv_conv1"], writes=[("rkv", j)])
                            yield
                            P.stt("dve", o3, zp4[:, j, :, 0:T], cv["conv0"][:, cc:cc + 1], o3, ALU.mult, ALU.add,
                                  reads=[TR[j], "cv_conv0", ("rkv", j)], writes=[("rkv", j)])
                            yield
                            P.stt("dve", o3, zp4[:, j, :, 2:T + 2], cv["conv2"][:, cc:cc + 1], o3, ALU.mult, ALU.add,
                                  reads=[TR[j], "cv_conv2", ("rkv", j)], writes=[("rkv", j)])
                            yield
                        P.ts("dve", kk, k_, cv["k_k"][:, hp:hp + 1], ALU.mult, reads=[("rkv", 1), "cv_k_k"], writes=["kk"])
                        yield
                        P.tt("pool", Tn[0], kk, kk, ALU.mult, reads=["kk"], writes=["T1"])
                        yield
                        for half in range(2):
                            hs = slice(half * 512, (half + 1) * 512)
                            P.pe([(psb[half], bones, Tn[0][:, hs], True, True, None)], reads=["T1", "ct"], writes=[PSR(half)])
                            yield
                            P.act(Tn[1][:, hs], psb[half], AF.Ln, reads=[PSR(half), "ct"], writes=["T2"], bias=eps12, scale=1.0)
                            yield
                        P.act(Tn[1], Tn[1], AF.Exp, reads=["T2"], writes=["T2"], scale=-0.5)
                        yield
                        P.tt("dve", kk, kk, Tn[1], ALU.mult, reads=["kk", "T2"], writes=["kk"])
                        yield
                        P.copy("act", fmKB[:, 0, :], v_, reads=[("rkv", 2)], writes=["fm2_0"])
                        yield
                        for half in range(2):
                            P.pe([mm for c in range(8 * half, 8 * half + 8) for mm in
                                  hd(lambda e, c=c: psb[half][sl(e), (c % 8) * 64:(c % 8) * 64 + 64],
                                     lambda e, c=c: fmKB[sl(e), 0, c * 64:(c + 1) * 64],
                                     lambda e: ident[sl(e), sl(e)], True, True)],
                                 reads=["fm2_0", "ident"], writes=[PSR(half)])
                            yield
                            P.copy("act", vtm[:, 8 * half:8 * half + 8, :], psb[half].rearrange("p (c v) -> p c v", c=8),
                                   reads=[PSR(half)], writes=[vtmR])
                            yield

                    def dirs():
                        def gen_prep(d):
                            fmT = FMV[d]; ktm = ktm_d[d]; nbtm = nbtm_d[d]; st0 = st0_d[d]; NTb = NTb_d[d]; gam = gam_d[d]
                            fm0R = f"fm0_{d}"; fm1R = f"fm1_{d}"; gamR = f"gam{d}"; ktmR = f"ktm{d}"; nbtmR = f"nbtm{d}"
                            fm2R = f"fm2_{d}"; fm3R = f"fm3_{d}"
                            FM_R = [fm0R, fm1R, fm2R, fm3R]
                            kdT = Tn[4 + d]
                            kdR = TR[4 + d]
                            for half in range(2):
                                hs = slice(half * 512, (half + 1) * 512)
                                P.pe([(psb[half], wsm[:, 2 + d, :], ad_m[:, hs], True, True, None)], reads=[wsmR, "ad_m"], writes=[PSR(half)])
                                yield
                                P.act(Tn[0][:, hs], psb[half], AF.Sigmoid, reads=[PSR(half), "cv_a0"], writes=["T1"],
                                      bias=cv["a0"][:, d * 16 + hp:d * 16 + hp + 1], scale=1.0)
                                yield
                            for half in range(2):
                                hs = slice(half * 512, (half + 1) * 512)
                                P.pe([(psb[half], wsm[:, d, :], twd_m[:, hs], True, True, None)], reads=[wsmR, "twd_m"], writes=[PSR(half)])
                                yield
                                P.act(Tn[1][:, hs], psb[half], AF.Sigmoid, reads=[PSR(half), "cv_w0"], writes=["T2"],
                                      bias=cv["w0"][:, d * 16 + hp:d * 16 + hp + 1], scale=1.0)
                                yield
                            a_, lw_, ci_, b_ = Tn[0], Tn[1], Tn[2], Tn[3]
                            P.ts("dve", lw_, lw_, float(-np.exp(-0.5)), ALU.mult, reads=["T2"], writes=["T2"])
                            yield
                            P.tt("pool", b_, kk, a_, ALU.mult, reads=["kk", "T1"], writes=["T4"])
                            yield
                            P.ts("dve", kdT, a_, cv["k_a"][:, hp:hp + 1], ALU.mult, s2=omka[:, hp:hp + 1], op1=ALU.add,
                                 reads=["T1", "cv_k_a", "omka"], writes=[kdR])
                            yield
                            P.tt("dve", kdT, kdT, k_, ALU.mult, reads=[kdR, ("rkv", 1)], writes=[kdR])
                            yield
                            S.op("dve", lambda e, ci_=ci_, lw_=lw_: e.tensor_tensor_scan(out=ci_, data0=scanmask, data1=lw_, initial=0.0,
                                                                                      op0=ALU.mult, op1=ALU.add),
                                 reads=["T2", "ct"], writes=["T3"])
                            yield
                            ci3 = ci_.rearrange("p (c t) -> p c t", t=64)
                            lw3 = lw_.rearrange("p (c t) -> p c t", t=64)
                            if d == 0:
                                P.copy("dve", gam, ci3[:, :, 63], reads=["T3"], writes=[gamR])
                                yield
                            else:
                                P.copy("dve", gam, ci3[:, :, 63], reads=["T3"], writes=[gamR])
                                yield
                                P.tt("dve", ci_, lw_, ci_, ALU.subtract, reads=["T2", "T3"], writes=["T3"])
                                yield
                                P.tt("dve", ci3, ci3, gam.unsqueeze(2).to_broadcast([128, NCH, 64]), ALU.add, reads=["T3", gamR], writes=["T3"])
                                yield
                            ce_ = Tn[0]
                            P.tt("dve", ce_, ci_, lw_, ALU.subtract, reads=["T3", "T2", "T4", kdR], writes=["T1"])
                            yield
                            P.act(ce_, ce_, AF.Exp, reads=["T1"], writes=["T1"])
                            yield
                            P.tt("dve", fmT[:, 0, :], kk, ce_, ALU.mult, reads=["kk", "T1", vtmR], writes=[fm0R])
                            yield
                            P.act(lw_, ci_, AF.Exp, reads=["T3", "T1"], writes=["T2"])
                            yield
                            P.tt("pool", fmT[:, 1, :], r_, lw_, ALU.mult, reads=[("rkv", 0), "T2"], writes=[fm1R])
                            yield
                            P.act(ce_, ci_, AF.Exp, reads=["T3", fm0R], writes=["T1"], scale=-1.0)
                            yield
                            P.tt("dve", fmT[:, 2, :], kdT, ce_, ALU.mult, reads=[kdR, "T1", vtmR], writes=[fm2R])
                            yield
                            P.tt("pool", fmT[:, 3, :], b_, ce_, ALU.mult, reads=["T4", "T1"], writes=[fm3R])
                            yield
                            P.act(gam, gam, AF.Exp, reads=[gamR], writes=[gamR])
                            yield
                            for (src, dst, dreg, sc) in ((2, ktm, ktmR, 1.0), (3, nbtm, nbtmR, -1.0)):
                                for half in range(2):
                                    P.pe([mm for c in range(8 * half, 8 * half + 8) for mm in
                                          hd(lambda e, c=c: psb[half][sl(e), (c % 8) * 64:(c % 8) * 64 + 64],
                                             lambda e, c=c, src=src: fmT[sl(e), src, c * 64:(c + 1) * 64],
                                             lambda e: ident[sl(e), sl(e)], True, True)],
                                         reads=[FM_R[src], "ident"], writes=[PSR(half)])
                                    yield
                                    P.act(dst[:, 8 * half:8 * half + 8, :], psb[half].rearrange("p (c v) -> p c v", c=8), AF.Copy,
                                          reads=[PSR(half)], writes=[dreg], scale=sc)
                                    yield

                        def gen_inv(d):
                            fmT = FMV[d]; ktm = ktm_d[d]; nbtm = nbtm_d[d]; st0 = st0_d[d]; NTb = NTb_d[d]; gam = gam_d[d]
                            fm0R = f"fm0_{d}"; fm1R = f"fm1_{d}"; gamR = f"gam{d}"; ktmR = f"ktm{d}"; nbtmR = f"nbtm{d}"
                            fm2R = f"fm2_{d}"; fm3R = f"fm3_{d}"
                            FM_R = [fm0R, fm1R, fm2R, fm3R]
                            YXa, YXb = YX[0], YX[1]
                            YXa, YXb = YX[0], YX[1]
                            for c in range(NCH):
                                bank = 2 + (c % 4)
                                ps = psb[bank]
                                cs = slice(c * 64, (c + 1) * 64)
                                ops = ((0, 3), (3, 0), (2, 0), (2, 1), (3, 1))
                                P.pe([mm for bi, (li, ri) in enumerate(ops) for mm in
                                      hd(lambda e, bi=bi: ps[sl(e), bi * 64:(bi + 1) * 64],
                                         lambda e, li=li: fmT[sl(e), li, cs],
                                         lambda e, ri=ri: fmT[sl(e), ri, cs], True, True)],
                                     reads=FM_R, writes=[PSR(bank)])
                                yield
                                P.tt("dve", YXa[:, c, :, :], ps[:, 0:128].rearrange("p (a b) -> p a b", a=2),
                                     maskd[d][:, 0:128].rearrange("p (a b) -> p a b", a=2), ALU.mult,
                                     reads=[PSR(bank), "ct"], writes=[("YX0", c)])
                                yield
                                P.tt("dve", st0[:, c, :, :], ps[:, 128:320].rearrange("p (a b) -> p a b", a=3),
                                     maskd[d][:, 128:320].rearrange("p (a b) -> p a b", a=3), ALU.mult,
                                     reads=[PSR(bank), "ct"], writes=[("st0", d, c)])
                                yield
                            YXR = [[("YX0", c) for c in range(NCH)], [("YX0", c) for c in range(NCH)]]
                            for hh in range(2):
                                c8 = slice(8 * hh, 8 * hh + 8)
                                P.tt("dve", NTf[:, c8, :], YXa[:, c8, 1, :], I2.unsqueeze(1).to_broadcast([128, 8, 64]), ALU.add,
                                     reads=YXR[0][8 * hh:8 * hh + 8] + ["I2"], writes=[("NTf", hh)])
                                yield
                            for hh in range(2):
                                c8 = slice(8 * hh, 8 * hh + 8)
                                P.copy("act", NTb[:, c8, :], NTf[:, c8, :], reads=[("NTf", hh)], writes=[("NTb", d, hh)])
                                yield
                            cur = 0
                            for lvl in range(1, 6):
                                old, new = YX[cur], YX[1 - cur]
                                oldR, newR = YXR[cur], YXR[1 - cur]
                                for hh in range(2):
                                    mms = []
                                    for c in range(8 * hh, 8 * hh + 8):
                                        pso = psb[2 + 2 * hh + (c % 8) // 4][:, (c % 4) * 128:(c % 4) * 128 + 128]
                                        mms += hd(lambda e, pso=pso: pso[sl(e), 0:64], lambda e, c=c: old[sl(e), c, 1, :], lambda e, c=c: old[sl(e), c, 0, :], True, True)
                                        if lvl < 5:
                                            mms += hd(lambda e, pso=pso: pso[sl(e), 64:128], lambda e, c=c: old[sl(e), c, 0, :], lambda e, c=c: old[sl(e), c, 1, :], True, True)
                                    P.pe(mms, reads=oldR[8 * hh:8 * hh + 8], writes=[PSR(2 + 2 * hh), PSR(3 + 2 * hh)])
                                    yield
                                for hh in range(2):
                                    for bk in range(2):
                                        c_lo = 8 * hh + 4 * bk
                                        src4 = psb[2 + 2 * hh + bk].rearrange("p (c a b) -> p c a b", c=4, a=2)
                                        eng_ = "act" if bk == 0 else "dve"
                                        if lvl < 5:
                                            P.copy(eng_, new[:, c_lo:c_lo + 4, :, :], src4,
                                                   reads=[PSR(2 + 2 * hh + bk)], writes=newR[c_lo:c_lo + 4])
                                            yield
                                        else:
                                            P.copy(eng_, new[:, c_lo:c_lo + 4, 0, :], src4[:, :, 0, :],
                                                   reads=[PSR(2 + 2 * hh + bk)], writes=newR[c_lo:c_lo + 4])
                                            yield
                                for hh in range(2):
                                    mms = []
                                    for c in range(8 * hh, 8 * hh + 8):
                                        mms += hd(lambda e, c=c: psb[6 + hh][sl(e), (c % 8) * 64:(c % 8) * 64 + 64], lambda e, c=c: new[sl(e), c, 0, :],
                                                  lambda e, c=c: NTb[sl(e), c, :], True, True)
                                    P.pe(mms, reads=newR[8 * hh:8 * hh + 8] + [("NTb", d, hh)], writes=[PSR(6 + hh)])
                                    yield
                                for hh in range(2):
                                    c8 = slice(8 * hh, 8 * hh + 8)
                                    P.tt("dve", NTf[:, c8, :], NTf[:, c8, :], psb[6 + hh].rearrange("p (c t) -> p c t", c=8), ALU.add,
                                         reads=[("NTf", hh), PSR(6 + hh)], writes=[("NTf", hh)])
                                    yield
                                for hh in range(2):
                                    c8 = slice(8 * hh, 8 * hh + 8)
                                    P.copy("act", NTb[:, c8, :], NTf[:, c8, :], reads=[("NTf", hh)], writes=[("NTb", d, hh)])
                                    yield
                                cur = 1 - cur

                        def interleave2(ga, gb):
                            gens = [g for g in (ga, gb) if g is not None]
                            while gens:
                                for g in list(gens):
                                    try:
                                        next(g)
                                    except StopIteration:
                                        gens.remove(g)
                        interleave2(gen_prep(0), None)
                        interleave2(gen_inv(0), gen_prep(1))
                        interleave2(gen_inv(1), None)
                        t4_ = Tn[3]
                        P.tt("pool", t4_, Tn[4], Tn[5], ALU.add, reads=["T5", "T6"], writes=["T4"])
                        P.stt("dve", t4_, r_, cv["r_k"][:, hp:hp + 1], t4_, ALU.mult, ALU.mult, reads=[("rkv", 0), "cv_r_k", "T4"], writes=["T4"])
                        for half in range(2):
                            hs = slice(half * 512, (half + 1) * 512)
                            P.pe([(psb[half], bones, t4_[:, hs], True, True, None)], reads=["T4", "ct"], writes=[PSR(half)])
                            P.tt("dve", t4_[:, hs], v_[:, hs], psb[half], ALU.mult, reads=[("rkv", 2), PSR(half), "T4"], writes=["T4"])

                    def gen_chains():
                        chains = []
                        for d in range(2):
                            for s_ in range(nseq):
                                cl = list(range(s_ * CPS, (s_ + 1) * CPS))
                                if d == 1:
                                    cl = cl[::-1]
                                chains.append((d, s_, d * nseq + s_, cl))
                        for (d, s_, ch, cl) in chains:
                            a0r = ("A0", ch)
                            if has_cache:
                                P.dma("sp", A0g[:, ch, :], st_in[d, 2 * hp:2 * hp + 2].rearrange("e v k -> (e v) k"), writes=[("A0g", ch)])
                                yield
                                P.pe(hd(lambda e: psb[4][sl(e), 0:64], lambda e: A0g[sl(e), ch, :], lambda e: identf[sl(e), sl(e)], True, True),
                                     reads=[("A0g", ch), "ct"], writes=[PSR(4)])
                                yield
                                P.copy("dve", A0[:, ch, :], psb[4][:, 0:64], reads=[PSR(4)], writes=[a0r])
                                yield
                            else:
                                P.memset("pool", A0[:, ch, :], 0.0, writes=[a0r])
                                yield
                            P.copy("act", A0b[:, ch, :], A0[:, ch, :], reads=[a0r], writes=[("A0b", ch)])
                            yield
                        for step in range(CPS):
                            info = []
                            for (d, s_, ch, cl) in chains:
                                c = cl[step]
                                if len(chains) <= 6:
                                    slot = 2 + ch
                                    bkc = psb[slot]
                                elif ch < 4:
                                    slot = 2 + ch
                                    bkc = psb[slot]
                                else:
                                    slot = 6 + (ch - 4) // 2
                                    bkc = psb[slot][:, (ch % 2) * 256:(ch % 2) * 256 + 256]
                                info.append((d, ch, c, slice(c * 64, (c + 1) * 64), slot, bkc[:, 0:64], bkc[:, 64:128], bkc[:, 128:192], bkc[:, 192:256],
                                             ("A0", ch), ("A0b", ch)))
                            for (d, ch, c, cs, slot, psR, psP, psY, psS, a0r, a0br) in info:
                                P.ts("dve", A0g[:, ch, :], A0[:, ch, :], gam_d[d][:, c:c + 1], ALU.mult, reads=[a0r, f"gam{d}"], writes=[("A0g", ch)])
                                yield
                                P.pe(hd(lambda e: psR[sl(e), :], lambda e: fmKR[d][sl(e), 0, cs], lambda e: A0b[sl(e), ch, :], True, False) +
                                     hd(lambda e: psR[sl(e), :], lambda e: st0_d[d][sl(e), c, 0, :], lambda e: vtm[sl(e), c, :], False, True),
                                     reads=[f"fm0_{d}", a0br, ("st0", d, c), vtmR], writes=[PSR(slot)])
                                yield
                            for (d, ch, c, cs, slot, psR, psP, psY, psS, a0r, a0br) in info:
                                P.copy("act", rhsb[:, ch, :], psR, reads=[PSR(slot)], writes=[("rhsb", ch)])
                                yield
                            for (d, ch, c, cs, slot, psR, psP, psY, psS, a0r, a0br) in info:
                                P.pe(hd(lambda e: psP[sl(e), :], lambda e: NTb_d[d][sl(e), c, :], lambda e: rhsb[sl(e), ch, :], True, True),
                                     reads=[("NTb", d, c // 8), ("rhsb", ch)], writes=[PSR(slot)])
                                yield
                            for (d, ch, c, cs, slot, psR, psP, psY, psS, a0r, a0br) in info:
                                P.copy("dve", pb[:, ch, :], psP, reads=[PSR(slot)], writes=[("pb", ch)])
                                yield
                            for (d, ch, c, cs, slot, psR, psP, psY, psS, a0r, a0br) in info:
                                P.pe(hd(lambda e: psY[sl(e), :], lambda e: A0b[sl(e), ch, :], lambda e: fmKR[d][sl(e), 1, cs], True, False) +
                                     hd(lambda e: psY[sl(e), :], lambda e: vtm[sl(e), c, :], lambda e: st0_d[d][sl(e), c, 1, :], False, False) +
                                     hd(lambda e: psY[sl(e), :], lambda e: pb[sl(e), ch, :], lambda e: st0_d[d][sl(e), c, 2, :], False, True) +
                                     hd(lambda e: psS[sl(e), :], lambda e: ktm_d[d][sl(e), c, :], lambda e: vtm[sl(e), c, :], True, False) +
                                     hd(lambda e: psS[sl(e), :], lambda e: nbtm_d[d][sl(e), c, :], lambda e: pb[sl(e), ch, :], False, True),
                                     reads=[a0br, f"fm1_{d}", vtmR, ("st0", d, c), ("pb", ch), f"ktm{d}", f"nbtm{d}"], writes=[PSR(slot)])
                                yield
                            for (d, ch, c, cs, slot, psR, psP, psY, psS, a0r, a0br) in info:
                                P.stt("dve", A0[:, ch, :], psS, gam_d[d][:, c:c + 1], A0g[:, ch, :], ALU.mult, ALU.add,
                                      reads=[PSR(slot), f"gam{d}", ("A0g", ch)], writes=[a0r])
                                yield
                            for (d, ch, c, cs, slot, psR, psP, psY, psS, a0r, a0br) in info:
                                P.copy("act", A0b[:, ch, :], A0[:, ch, :], reads=[a0r], writes=[a0br])
                                yield
                            for (d, ch, c, cs, slot, psR, psP, psY, psS, a0r, a0br) in info:
                                P.tt("dve", ysum[:, cs], ysum[:, cs], psY, ALU.add, reads=[PSR(slot), ("ysum", c), "ysum"], writes=[("ysum", c)])
                                yield
                        if gi == 0:
                            for (d, s_, ch, cl) in chains:
                                P.pe(hd(lambda e: psb[2 + ch % 6][sl(e), 0:64], lambda e: A0[sl(e), ch, :],
                                        lambda e: identf[sl(e), sl(e)], True, True),
                                     reads=[("A0", ch), "ct"], writes=[PSR(2 + ch % 6)])
                                yield
                                P.copy("dve", A0g[:, ch, :], psb[2 + ch % 6][:, 0:64],
                                       reads=[PSR(2 + ch % 6)], writes=[("A0g", ch)])
                                yield
                                P.dma("sp", nst[s_, d, 2 * hp:2 * hp + 2].rearrange("e v k -> (e v) k"), A0g[:, ch, :],
                                      reads=[("A0g", ch)])
                                yield

                    def groupnorm():
                        YS_R = [("ysum", c) for c in range(NCH)] + ["ysum"]
                        mean_, d_, t3_ = Tn[4], Tn[5], gn_tmp
                        GT_R = ["fm2_1", "fm3_1"]
                        for half in range(2):
                            hs = slice(half * 512, (half + 1) * 512)
                            P.pe([(psb[half], bavg, ysum[:, hs], True, True, None)], reads=YS_R + ["ct"], writes=[PSR(half)])
                            P.tt("dve", d_[:, hs], ysum[:, hs], psb[half], ALU.subtract, reads=YS_R + [PSR(half)], writes=["T6"])
                        P.tt("pool", t3_, d_, d_, ALU.mult, reads=["T6"], writes=GT_R)
                        for half in range(2):
                            hs = slice(half * 512, (half + 1) * 512)
                            P.pe([(psb[half], bavg, t3_[:, hs], True, True, None)], reads=GT_R + ["ct"], writes=[PSR(half)])
                            P.act(mean_[:, hs], psb[half], AF.Ln, reads=[PSR(half), "ct"], writes=["T5"], bias=epslnx, scale=1.0)
                        P.act(mean_, mean_, AF.Exp, reads=["T5"], writes=["T5"], scale=-0.5)
                        P.tt("dve", d_, d_, mean_, ALU.mult, reads=["T6", "T5"], writes=["T6"])
                        P.ts("dve", d_, d_, cv["lnx_w"][:, hp:hp + 1], ALU.mult, s2=cv["lnx_b"][:, hp:hp + 1], op1=ALU.add,
                             reads=["T6", "cv_lnx_w", "cv_lnx_b"], writes=["T6"])
                        P.tt("dve", d_, d_, Tn[3], ALU.add, reads=["T6", "T4"], writes=["T6"])
                        for half in range(2):
                            hs = slice(half * 512, (half + 1) * 512)
                            P.pe([(psb[half], wsm[:, 4, :], sgdT[:, hs], True, True, None)], reads=[wsmR, "sgdT"], writes=[PSR(half)])
                            P.tt("dve", orwkvT[:, hp, hs], d_[:, hs], psb[half], ALU.mult, reads=["T6", PSR(half)], writes=[("orwkvT", hp)])
                        P.memset("pool", ysum, 0.0, writes=["ysum"])

                    return gen_z, dirs, gen_chains, groupnorm

                def interleaveN(ga, gb):
                    gens = [g for g in (ga, gb) if g is not None]
                    while gens:
                        for g in list(gens):
                            try:
                                next(g)
                            except StopIteration:
                                gens.remove(g)
                ctxs = [hp_ctx(hp) for hp in range(16)]
                P.memset("pool", ysum, 0.0, writes=["ysum"])
                interleaveN(ctxs[0][0](), None)
                load_hp_weights(1)
                for hp in range(16):
                    ctxs[hp][1]()
                    interleaveN(ctxs[hp][2](), ctxs[hp + 1][0]() if hp + 1 < 16 else None)
                    ctxs[hp][3]()
                    if hp + 2 < 16:
                        load_hp_weights(hp + 2)
                P.release(mR)

            phase_R()
            OR_R = [("orwkvT", hp) for hp in range(16)]
            P.dbg(f"orwkvT{gi}", orwkvT, [16, NT], OR_R)
            if stop_after == "R":
                return

            omlaT = P.alloc(BF16, [16, NT])
            m_kv = P.mark()
            NKT = 10 if has_cache else 8
            ckvnT = P.alloc(BF16, [4, NKT * 128])
            krgT = P.alloc(F32, [NKT * 128])
            sqKR = P.alloc(BF16, [NKT * 128])
            mB = P.mark()
            wck = P.alloc(BF16, [KC, 576])
            P.dma("pool", wck, w_in[:, OFF_CKV:OFF_CKV + 576].rearrange("(kc p) n -> p kc n", p=128), writes=["wck"])
            zc = [P.alloc(F32, [576]) for _ in range(2)]
            cn = [P.alloc(BF16, [512]) for _ in range(2)]
            sq = P.alloc(F32, [2 * NKT])
            junkb = P.alloc(BF16, [512])
            tmpr = P.alloc(F32, [2, 64])
            for kt in range(NKT):
                zt = zc[kt % 2]
                zr = ("zc", kt % 2)
                if kt < 8:
                    P.pe([(psb[2][:, 0:512], hT[:, kc, kt * 128:(kt + 1) * 128], wck[:, kc, 0:512], kc == 0, kc == KC - 1, None)
                          for kc in range(KC)] +
                         [(psb[3][:, 0:64], hT[:, kc, kt * 128:(kt + 1) * 128], wck[:, kc, 512:576], kc == 0, kc == KC - 1, None)
                          for kc in range(KC)], reads=["wck", ("hT", kt)], writes=[PSR(2), PSR(3)])
                    P.copy("act", zt[:, 0:512], psb[2][:, 0:512], reads=[PSR(2)], writes=[zr])
                    P.copy("dve", zt[:, 512:576], psb[3][:, 0:64], reads=[PSR(3)], writes=[zr])
                    if gi == 0:
                        P.dma("sp", nckv[kt * 128:(kt + 1) * 128, :], zt[:, 0:512], reads=[zr])
                        P.dma("sp", nkr[kt * 128:(kt + 1) * 128, :], zt[:, 512:576], reads=[zr])
                else:
                    P.dma("sp", zt[:, 0:512], cckv[(kt - 8) * 128:(kt - 7) * 128, :], writes=[zr])
                    P.dma("sp", zt[:, 512:576], ckr[(kt - 8) * 128:(kt - 7) * 128, :], writes=[zr])
                P.act(junkb, zt[:, 0:512], AF.Square, reads=[zr], writes=[("sqB", kt)], accum_out=sq[:, kt:kt + 1])
                P.act(sq[:, NKT + kt:NKT + kt + 1], sq[:, kt:kt + 1], AF.Sqrt, reads=[("sqB", kt), "ct"], writes=[("sqB2", kt)],
                      bias=eps1, scale=float(1.0 / 512.0))
                P.recip(sq[:, NKT + kt:NKT + kt + 1], sq[:, NKT + kt:NKT + kt + 1], reads=[("sqB2", kt)], writes=[("sqB2", kt)])
                cb = cn[kt % 2]
                P.ts("dve", cb, zt[:, 0:512], sq[:, NKT + kt:NKT + kt + 1], ALU.mult, reads=[zr, ("sqB2", kt)], writes=[("cn", kt % 2)])
                pst = psb[4]
                P.pe([(pst[:, j * 128:(j + 1) * 128], cb[:, j * 128:(j + 1) * 128], ident, True, True, None) for j in range(4)],
                     reads=[("cn", kt % 2), "ident"], writes=[PSR(4)])
                for j in range(4):
                    P.act(ckvnT[:, j, kt * 128:(kt + 1) * 128], pst[:, j * 128:(j + 1) * 128], AF.Copy,
                          reads=[PSR(4), "cv_kv_norm"], writes=[("ckvnT", kt)], scale=cv["kv_norm"][:, j:j + 1])
            krT = P.alloc(F32, [512])
            rtm_ = P.alloc(F32, [512])
            NKH = (NKT * 128 + 511) // 512
            for kh in range(NKH):
                k0 = kh * 512
                n_ = min(512, NKT * 128 - k0)
                ks = slice(k0, k0 + n_)
                if kh < 2:
                    P.pe([(psb[5][0:64, 0:512], wck[:, kc, 512:576], hT[:, kc, ks], kc == 0, kc == KC - 1, None) for kc in range(KC)],
                         reads=["wck"] + HT_R, writes=[PSR(5)])
                else:
                    P.pe([(psb[5][0:64, (kt - 8) * 128:(kt - 7) * 128], zc[kt % 2][:, 512:576], identf, True, True, None)
                          for kt in range(8, NKT)], reads=[("zc", 0), ("zc", 1), "ct"], writes=[PSR(5)])
                P.act(sqKR[0:64, ks], psb[5][0:64, 0:n_], AF.Square, reads=[PSR(5)], writes=["sqKR"])
                if has_cache and kh < 2:
                    P.act(krT[0:64, 0:n_], psb[5][0:64, 0:n_], AF.Copy, reads=[PSR(5), "gcol"], writes=["krT"], scale=gcol[0:64, 3:4])
                    P.pe([(psb[6][0:64, 0:n_], permf, krT[0:64, 0:n_], True, True, None)], reads=["krT", "ct"], writes=[PSR(6)])
                    P.tt("dve", rtm_[0:64, 0:n_], psb[6][0:64, 0:n_], ropeST[:, ks], ALU.mult, reads=[PSR(6), "ct"], writes=["rtm_"])
                    P.tt("dve", krT[0:64, 0:n_], krT[0:64, 0:n_], ropeCT[:, ks], ALU.mult, reads=["krT", "ct"], writes=["krT"])
                    P.tt("dve", krgT[0:64, ks], krT[0:64, 0:n_], rtm_[0:64, 0:n_], ALU.add, reads=["krT", "rtm_"], writes=["krgT"])
                else:
                    P.act(krgT[0:64, ks], psb[5][0:64, 0:n_], AF.Copy, reads=[PSR(5), "gcol"], writes=["krgT"], scale=gcol[0:64, 3:4])
            P.release(mB)
            P.dbg(f"ckvnT{gi}", ckvnT, [4, NKT * 128], [("ckvnT", kt) for kt in range(NKT)])
            P.dbg(f"krgT{gi}", krgT, [NKT * 128], ["krgT"])
            if stop_after == "B":
                return


            def phase_T():
                mT = P.mark()
                NK = NKT * 128
                wq = [P.alloc(BF16, [KC, 192]) for _ in range(2)]
                wkv = [P.alloc(BF16, [4, 256]) for _ in range(2)]
                qT = [P.alloc(BF16, [NT]) for _ in range(2)]; qTr = [P.alloc(BF16, [NT]) for _ in range(2)]
                kT = [P.alloc(BF16, [NK]) for _ in range(2)]; kTr = [P.alloc(BF16, [NK]) for _ in range(2)]
                Vt = [P.alloc(BF16, [NKT, 128]) for _ in range(2)]
                sqA = P.alloc(BF16, [512]); sqB = P.alloc(BF16, [512])
                rs = P.alloc(F32, [512])
                xg = P.alloc(F32, [512]); xr = P.alloc(F32, [512])
                pT = [P.alloc(BF16, [512]) for _ in range(2)]
                rl = P.alloc(F32, [512])

                def prep(h):
                    hb = h % 2
                    wqb = wq[hb]; wkb = wkv[hb]
                    P.dma("pool", wqb, w_in[:, OFF_Q + h * 192:OFF_Q + (h + 1) * 192].rearrange("(kc p) n -> p kc n", p=128),
                          writes=[("wq", hb)])
                    P.dma("pool", wkb, w_kv_up[:, h * 256:(h + 1) * 256].rearrange("(j p) n -> p j n", p=128), writes=[("wkv", hb)])
                    yield
                    for half in range(2):
                        hs = slice(half * 512, (half + 1) * 512)
                        P.pe([(psb[0], wqb[:, kc, 0:128], hT[:, kc, hs], kc == 0, kc == KC - 1, None) for kc in range(KC)],
                             reads=[("wq", hb)] + HT_R, writes=[PSR(0)])
                        yield
                        P.pe([(psb[1][0:64, :], wqb[:, kc, 128:192], hT[:, kc, hs], kc == 0, kc == KC - 1, None) for kc in range(KC)],
                             reads=[("wq", hb)] + HT_R, writes=[PSR(1)])
                        P.act(sqA, psb[0], AF.Square, reads=[PSR(0)], writes=["sqA"])
                        yield
                        P.act(sqB[0:64, :], psb[1][0:64, :], AF.Square, reads=[PSR(1)], writes=["sqB"])
                        P.pe([(psb[2], ones_bf, sqA, True, False, None), (psb[2], ones_bf[0:64, :], sqB[0:64, :], False, True, None)],
                             reads=["sqA", "sqB", "ones_bf"], writes=[PSR(2)])
                        yield
                        P.act(rs, psb[2], AF.Ln, reads=[PSR(2), "ct"], writes=["rs"], bias=eps1, scale=float(1.0 / 192.0))
                        P.act(rs, rs, AF.Exp, reads=["rs"], writes=["rs"], scale=-0.5)
                        yield
                        P.stt("dve", qT[hb][:, hs], psb[0], gcol[:, 0:1], rs, ALU.mult, ALU.mult, reads=[PSR(0), "gcol", "rs"], writes=[("qT", hb, half)])
                        if has_cache:
                            P.act(xg[0:64, :], psb[1][0:64, :], AF.Copy, reads=[PSR(1), "gcol"], writes=["xg"], scale=gcol[0:64, 1:2])
                            P.pe([(psb[3][0:64, :], permf, xg[0:64, :], True, True, None)], reads=["xg", "ct"], writes=[PSR(3)])
                            yield
                            P.tt("dve", xr[0:64, :], psb[3][0:64, :], ropeST[:, hs], ALU.mult, reads=[PSR(3), "ct"], writes=["xr"])
                            P.tt("dve", xg[0:64, :], xg[0:64, :], ropeCT[:, hs], ALU.mult, reads=["xg", "ct"], writes=["xg"])
                            yield
                            P.tt("dve", xg[0:64, :], xg[0:64, :], xr[0:64, :], ALU.add, reads=["xg", "xr"], writes=["xg"])
                            P.tt("dve", qTr[hb][0:64, hs], xg[0:64, :], rs[0:64, :], ALU.mult, reads=["xg", "rs"], writes=[("qT", hb, half)])
                        else:
                            P.stt("dve", qTr[hb][0:64, hs], psb[1][0:64, :], gcol[0:64, 1:2], rs[0:64, :], ALU.mult, ALU.mult,
                                  reads=[PSR(1), "gcol", "rs"], writes=[("qT", hb, half)])
                        yield
                    for kh in range((NK + 511) // 512):
                        k0 = kh * 512
                        n_ = min(512, NK - k0)
                        ks = slice(k0, k0 + n_)
                        kt_r = [("ckvnT", kt) for kt in range(k0 // 128, (k0 + n_) // 128)]
                        P.pe([(psb[0][:, 0:n_], wkb[:, j, 0:128], ckvnT[:, j, ks], j == 0, j == 3, None) for j in range(4)],
                             reads=[("wkv", hb)] + kt_r, writes=[PSR(0)])
                        P.act(sqA[:, 0:n_], psb[0][:, 0:n_], AF.Square, reads=[PSR(0)], writes=["sqA"])
                        yield
                        P.pe([(psb[2][:, 0:n_], ones_bf, sqA[:, 0:n_], True, False, None),
                              (psb[2][:, 0:n_], ones_bf[0:64, :], sqKR[0:64, ks], False, True, None)],
                             reads=["sqA", "sqKR", "ones_bf"], writes=[PSR(2)])
                        P.act(rs[:, 0:n_], psb[2][:, 0:n_], AF.Ln, reads=[PSR(2), "ct"], writes=["rs"], bias=eps1, scale=float(1.0 / 192.0))
                        yield
                        P.act(rs[:, 0:n_], rs[:, 0:n_], AF.Exp, reads=["rs"], writes=["rs"], scale=-0.5)
                        P.stt("dve", kT[hb][:, ks], psb[0][:, 0:n_], gcol[:, 2:3], rs[:, 0:n_], ALU.mult, ALU.mult,
                              reads=[PSR(0), "gcol", "rs"], writes=[("kT", hb, kh)])
                        P.tt("dve", kTr[hb][0:64, ks], krgT[0:64, ks], rs[0:64, 0:n_], ALU.mult, reads=["krgT", "rs"], writes=[("kT", hb, kh)])
                        yield
                        for kt in range(k0 // 128, (k0 + n_) // 128):
                            pv = psb[1] if kt % 2 == 0 else psb[3]
                            P.pe([(pv[:, 0:128], ckvnT[:, j, kt * 128:(kt + 1) * 128], wkb[:, j, 128:256], j == 0, j == 3, None) for j in range(4)],
                                 reads=[("wkv", hb), ("ckvnT", kt)], writes=[PSR(1 if kt % 2 == 0 else 3)])
                            P.copy("act", Vt[hb][:, kt, :], pv[:, 0:128], reads=[PSR(1 if kt % 2 == 0 else 3)], writes=[("Vt", hb, kt)])
                            yield

                def attn(h):
                    hb = h % 2
                    if has_cache:
                        blocks = [(qb * 512, 512, list(range(NKT))) for qb in range(2)]
                    else:
                        blocks = [(s_ * 256, 256, [2 * s_, 2 * s_ + 1]) for s_ in range(4)]
                    for bi, (q0, nq, kts) in enumerate(blocks):
                        qs = slice(q0, q0 + nq)
                        q_r = [("qT", hb, q0 // 512)]
                        po = psb[6]; pl = psb[7]
                        for ki, kt in enumerate(kts):
                            pS = psb[4 + ki % 2]
                            P.pe([(pS[:, 0:nq], kT[hb][:, kt * 128:(kt + 1) * 128], qT[hb][:, qs], True, False, None),
                                  (pS[:, 0:nq], kTr[hb][0:64, kt * 128:(kt + 1) * 128], qTr[hb][0:64, qs], False, True, None)],
                                 reads=[("kT", hb, kt // 4)] + q_r, writes=[PSR(4 + ki % 2)])
                            pt_ = pT[ki % 2]
                            P.act(pt_[:, 0:nq], pS[:, 0:nq], AF.Exp, reads=[PSR(4 + ki % 2)], writes=[("pT", ki % 2)])
                            P.pe([(po[:, 0:nq], Vt[hb][:, kt, :], pt_[:, 0:nq], ki == 0, ki == len(kts) - 1, None),
                                  (pl[:, 0:nq], ones_bf, pt_[:, 0:nq], ki == 0, ki == len(kts) - 1, None)],
                                 reads=[("Vt", hb, kt), ("pT", ki % 2), "ones_bf"], writes=[PSR(6), PSR(7)])
                            yield
                        P.act(rl[:, 0:nq], pl[:, 0:nq], AF.Ln, reads=[PSR(7)], writes=["rl"])
                        P.act(rl[:, 0:nq], rl[:, 0:nq], AF.Exp, reads=["rl"], writes=["rl"], scale=-1.0)
                        P.tt("dve", omlaT[:, h, qs], po[:, 0:nq], rl[:, 0:nq], ALU.mult, reads=[PSR(6), "rl"], writes=[("omlaT", h)])
                        yield

                def interleave(ga, gb):
                    gens = [g for g in (ga, gb) if g is not None]
                    while gens:
                        for g in list(gens):
                            try:
                                next(g)
                            except StopIteration:
                                gens.remove(g)

                interleave(prep(0), None)
                for h in range(H_MLA):
                    interleave(attn(h), prep(h + 1) if h + 1 < H_MLA else None)
                P.release(mT)

            phase_T()
            P.release(m_kv)
            OM_R = [("omlaT", h) for h in range(16)]
            P.dbg(f"omlaT{gi}", omlaT, [16, NT], OM_R)
            if stop_after == "T":
                return

            P.limit = ARENA_WORDS - 8192
            assert P.top <= P.limit
            mergedT = P.arena[:, P.limit:P.limit + 8192].bitcast(BF16).rearrange("p (a b) -> p a b", a=16)

            def phase_M():
                mM = P.mark()
                wm = [[P.alloc(BF16, [KC, 128]) for _ in range(4)] for _ in range(2)]
                sg = [P.alloc(F32, [512]) for _ in range(2)]
                it = 0
                for fb in range(16):
                    wset = wm[fb % 2]
                    srcs = (w_in[:, OFF_GATE + fb * 128:OFF_GATE + (fb + 1) * 128],
                            w_in[:, OFF_GATE + 2048 + fb * 128:OFF_GATE + 2048 + (fb + 1) * 128],
                            w_br_mla[:, fb * 128:(fb + 1) * 128], w_br_rwkv[:, fb * 128:(fb + 1) * 128])
                    for wi in range(4):
                        P.dma("pool", wset[wi], srcs[wi].rearrange("(kc p) n -> p kc n", p=128), writes=[("wm", fb % 2, wi)])
                    for half in range(2):
                        hs = slice(half * 512, (half + 1) * 512)
                        pb0 = 4 * (it % 2)
                        acts = (hT, hT, omlaT, orwkvT)
                        areg = (HT_R, HT_R, OM_R, OR_R)
                        for wi in range(4):
                            P.pe([(psb[pb0 + wi], wset[wi][:, kc, :], acts[wi][:, kc, hs], kc == 0, kc == KC - 1, None) for kc in range(KC)],
                                 reads=[("wm", fb % 2, wi)] + areg[wi], writes=[PSR(pb0 + wi)])
                        for wi in range(2):
                            P.act(sg[wi], psb[pb0 + wi], AF.Sigmoid, reads=[PSR(pb0 + wi)], writes=[("sg", wi)])
                            P.tt("dve", sg[wi], psb[pb0 + 2 + wi], sg[wi], ALU.mult, reads=[PSR(pb0 + 2 + wi), ("sg", wi)], writes=[("sg", wi)])
                        P.tt("dve", mergedT[:, fb, hs], sg[0], sg[1], ALU.add, reads=[("sg", 0), ("sg", 1)], writes=[("mergedT", fb)])
                        it += 1
                P.release(mM)

            phase_M()
            MG_R = [("mergedT", fb) for fb in range(16)]
            P.dbg(f"mergedT{gi}", mergedT, [16, NT], MG_R)
            if stop_after == "M":
                return

            def gate_bc(dst, sixth):
                onesf = P.alloc(F32, [128])
                dg = [P.alloc(F32, [128]) for _ in range(2)]
                P.memset("pool", onesf, 1.0, writes=["onesf"])
                for blk in range(16):
                    d_ = dg[blk % 2]
                    P.ts("dve", d_, identf, modT[:, sixth * 16 + blk, cj:cj + 1], ALU.mult, reads=["ct", "modT"], writes=[("dg", blk % 2)])
                    bank = (blk // 4) % 2
                    P.pe([(psb[bank][:, (blk % 4) * 128:(blk % 4) * 128 + 128], onesf, d_, True, True, None)],
                         reads=["onesf", ("dg", blk % 2)], writes=[PSR(bank)])
                    if blk % 4 == 3:
                        cb = blk // 4
                        P.copy("act", dst[:, cb * 512:(cb + 1) * 512], psb[bank], reads=[PSR(bank)], writes=["gatebc"])

            def phase_O():
                mO = P.mark()
                g1bc = P.alloc(F32, [D])
                wo = P.alloc(BF16, [KC, D])
                for cb in range(4):
                    P.dma("pool", wo[:, :, cb * 512:(cb + 1) * 512], w_out[:, cb * 512:(cb + 1) * 512].rearrange("(kc p) n -> p kc n", p=128),
                          writes=[("wo", cb)])
                gate_bc(g1bc, 2)
                xb2 = [P.alloc(F32, [D]) for _ in range(2)]
                x1b = [P.alloc(F32, [D]) for _ in range(2)]

                def x1_tile(ti):
                    xt = xb2[ti % 2]; x1 = x1b[ti % 2]
                    P.dma("sp", xt, xg[ti * 128:(ti + 1) * 128, :], writes=[("xb2", ti % 2)])
                    for cb in range(4):
                        pb_ = 4 + cb
                        P.pe([(psb[pb_], mergedT[:, kc, ti * 128:(ti + 1) * 128], wo[:, kc, cb * 512:(cb + 1) * 512], kc == 0, kc == KC - 1, None)
                              for kc in range(KC)], reads=[("wo", cb)] + MG_R, writes=[PSR(pb_)])
                        cs_ = slice(cb * 512, (cb + 1) * 512)
                        P.tt("dve", x1[:, cs_], psb[pb_], g1bc[:, cs_], ALU.mult, reads=[PSR(pb_), "gatebc"], writes=[("x1b", ti % 2)])
                        P.tt("pool", x1[:, cs_], x1[:, cs_], xt[:, cs_], ALU.add, reads=[("x1b", ti % 2), ("xb2", ti % 2)], writes=[("x1b", ti % 2)])
                    P.dma("sp", yg[ti * 128:(ti + 1) * 128, :], x1, reads=[("x1b", ti % 2)], writes=[("yscr", ti)])
                    return x1, ("x1b", ti % 2)
                norm_to_T(x1_tile, g2[:, cj, :], sh2[:, cj, :], hT, "hT")
                P.release(mO)

            P.release(m_h)
            phase_O()
            P.dbg(f"h2T{gi}", hT, [KC, NT], HT_R)
            if stop_after == "O":
                return

            def phase_F():
                mF = P.mark()
                g2bc = P.alloc(F32, [D])
                acc = P.alloc(F32, [8, D])
                w1 = [P.alloc(BF16, [KC, 512]) for _ in range(2)]
                w2 = P.alloc(BF16, [4, D])
                uT = P.alloc(BF16, [4, NT])
                rtm = [P.alloc(F32, [512]) for _ in range(2)]

                def load_ff(hb_):
                    P.dma("pool", w1[hb_ % 2], w_ff_in[:, hb_ * 512:(hb_ + 1) * 512].rearrange("(kc p) n -> p kc n", p=128), writes=[("w1", hb_ % 2)])
                    P.dma("pool", w2, w_ff_out[hb_ * 512:(hb_ + 1) * 512, :].rearrange("(j p) n -> p j n", p=128), writes=["w2"])
                load_ff(0)
                gate_bc(g2bc, 5)
                for hb in range(16):
                    if hb > 0:
                        load_ff(hb)
                    it = 0
                    for j in range(4):
                        for half in range(2):
                            hs = slice(half * 512, (half + 1) * 512)
                            pb_ = it % 2
                            P.pe([(psb[pb_], w1[hb % 2][:, kc, j * 128:(j + 1) * 128], hT[:, kc, hs], kc == 0, kc == KC - 1, None) for kc in range(KC)],
                                 reads=[("w1", hb % 2)] + HT_R, writes=[PSR(pb_)])
                            P.act(rtm[it % 2], psb[pb_], AF.Relu, reads=[PSR(pb_)], writes=[("rtm", it % 2)])
                            P.act(uT[:, j, hs], rtm[it % 2], AF.Square, reads=[("rtm", it % 2)], writes=[("uT", j, half)])
                            it += 1
                    it = 0
                    for ti in range(8):
                        for cb in range(4):
                            pb_ = 2 + it % 6
                            cs_ = slice(cb * 512, (cb + 1) * 512)
                            P.pe([(psb[pb_], uT[:, j, ti * 128:(ti + 1) * 128], w2[:, j, cs_], j == 0, j == 3, None) for j in range(4)],
                                 reads=["w2"] + [("uT", j, ti // 4) for j in range(4)], writes=[PSR(pb_)])
                            if hb == 0:
                                P.copy("dve", acc[:, ti, cs_], psb[pb_], reads=[PSR(pb_)], writes=[("acc", ti, cb)])
                            else:
                                P.tt("dve", acc[:, ti, cs_], acc[:, ti, cs_], psb[pb_], ALU.add, reads=[PSR(pb_), ("acc", ti, cb)], writes=[("acc", ti, cb)])
                            it += 1
                xb3_ = P.alloc(F32, [D])
                xb3 = [xb3_, xb3_]
                for ti in range(8):
                    xt = xb3[ti % 2]
                    P.dma("sp", xt, yg[ti * 128:(ti + 1) * 128, :], reads=[("yscr", ti)], writes=[("xb3", 0)])
                    P.tt("dve", acc[:, ti, :], acc[:, ti, :], g2bc, ALU.mult, reads=[("acc", ti, cb) for cb in range(4)] + ["gatebc"],
                         writes=[("acc", ti, cb) for cb in range(4)])
                    P.tt("pool", xt, xt, acc[:, ti, :], ALU.add, reads=[("xb3", 0)] + [("acc", ti, cb) for cb in range(4)], writes=[("xb3", 0)])
                    P.dma("sp", yg[ti * 128:(ti + 1) * 128, :], xt, reads=[("xb3", 0), ("yscr", ti)], writes=[("yscr", ti)])
                P.release(mF)

            P.release(m_h)
            P.limit = ARENA_WORDS
            phase_F()

        def rope64(src, dst, tmp, tile, reads, writes):
            Cc = ropeC[:, tile, :]
            Sg = ropeS[:, tile, :]
            s4 = src.rearrange("p (a h d) -> p a h d", a=2, h=2)
            t4 = tmp.rearrange("p (a h d) -> p a h d", a=2, h=2)
            g4 = Sg.rearrange("p (a h d) -> p a h d", a=2, h=2)
            P.tt("dve", t4[:, :, 0, :], s4[:, :, 1, :], g4[:, :, 0, :], ALU.mult, reads=list(reads) + ["ct"], writes=["ropetmp"])
            P.tt("dve", t4[:, :, 1, :], s4[:, :, 0, :], g4[:, :, 1, :], ALU.mult, reads=list(reads) + ["ct", "ropetmp"], writes=["ropetmp"])
            P.tt("dve", dst, src, Cc, ALU.mult, reads=list(reads) + ["ct"], writes=list(writes))
            P.tt("dve", dst, dst, tmp, ALU.add, reads=["ropetmp"] + list(writes), writes=list(writes))

        for gi in range(2):
            run_group(gi)
            S.barrier()

        S.finish("sp")
        with nc.Block() as block:
            S.emit(sems, block)
    return P


_CACHE = {}


def make_in_maps(inputs):
    f = lambda a: np.ascontiguousarray(np.asarray(a, dtype=np.float32))
    ct = make_consts()
    shared = {
        "ctab": ct,
        "norm1": f(inputs["norm1"][0]), "w_ada": f(inputs["w_ada"][0]), "b_ada": f(inputs["b_ada"][0]),
        "w_in": f(inputs["w_in"][0]), "q_norm": f(inputs["q_norm"][0]), "kv_norm": f(inputs["kv_norm"][0]),
        "w_kv_up": f(inputs["w_kv_up"][0]), "k_norm": f(inputs["k_norm"][0]), "conv_rkv": f(inputs["conv_rkv"][0]),
        "k_k": f(inputs["k_k"][0]), "k_a": f(inputs["k_a"][0]), "r_k": f(inputs["r_k"][0].reshape(-1)),
        "w0": f(inputs["w0"][0]), "w_up": f(inputs["w_up"][0]), "a0": f(inputs["a0"][0]), "a_up": f(inputs["a_up"][0]),
        "g_up": f(inputs["g_up"][0]), "lnx_w": f(inputs["lnx_w"][0]), "lnx_b": f(inputs["lnx_b"][0]),
        "w_br_mla": f(inputs["w_br_mla"][0]), "w_br_rwkv": f(inputs["w_br_rwkv"][0]), "w_out": f(inputs["w_out"][0]),
        "norm2": f(inputs["norm2"][0]), "w_ff_in": f(inputs["w_ff_in"][0]), "w_ff_out": f(inputs["w_ff_out"][0]),
    }
    maps = []
    for i in range(8):
        m = dict(shared)
        m["xp"] = f(inputs["x_prompt"][4 * i:4 * i + 4].reshape(NT, D))
        m["xs"] = f(inputs["x_sample"][i])
        m["cckv"] = f(inputs["cache_mla_ckv"][i, 0])
        m["ckr"] = f(inputs["cache_mla_kr"][i, 0])
        m["st"] = f(inputs["state_rwkv"][i, 0])
        m["cond"] = f(np.stack([inputs["c_ctx"], inputs["c"][i]], axis=0))
        maps.append(m)
    return maps


def kernel(**inputs):
    if "prog" not in _CACHE:
        _CACHE["prog"] = build_program()
    P = _CACHE["prog"]
    in_maps = make_in_maps(inputs)
    res = run_bass_kernel_spmd(P.nc, in_maps, core_ids=list(range(8)))
    r = res.results
    yp = np.concatenate([r[i]["yp"].reshape(4, 256, D) for i in range(8)], axis=0)
    ys = np.stack([r[i]["ys"] for i in range(8)], axis=0)
    nckv = np.concatenate([r[i]["nckv"].reshape(4, 1, 256, 512) for i in range(8)], axis=0)
    nkr = np.concatenate([r[i]["nkr"].reshape(4, 1, 256, 64) for i in range(8)], axis=0)
    nst = np.concatenate([r[i]["nst"].reshape(4, 1, 2, 32, 64, 64) for i in range(8)], axis=0)
    return (yp.astype(np.float32), ys.astype(np.float32), nckv.astype(np.float32), nkr.astype(np.float32),
            nst.astype(np.float32))
```
